# Optimizing a Trainium2 kernel written in Bass

```python
import math
import jax, jax.numpy as jnp
from jax import lax
import numpy as np

D_MODEL = 1024
BATCH = 4
SEQ = 8192
DEPTH = 2

HEAD_DIM = 64
MEM_LEN = 256
MEM_HEADS = 4
MEM_WIDTH = MEM_HEADS * HEAD_DIM
MIX_WIDTH = D_MODEL
TOK_WIDTH = MIX_WIDTH - MEM_WIDTH
FOX_HEADS = TOK_WIDTH // HEAD_DIM
GMLP_GROUPS = TOK_WIDTH // HEAD_DIM
CHUNK = 128
Q_BLOCK = 128
D_FF = 2816
N_MIXERS = 2
N_FOX = (DEPTH + 1) // 2
N_GMLP = DEPTH // 2
FOX_IN = 3 * TOK_WIDTH + FOX_HEADS + MEM_WIDTH
GMLP_IN = 2 * TOK_WIDTH + MEM_WIDTH
EPS = 1e-6

kernel_name = "hybrid_fox_gmlp_macaron_memxattn"


def rms_norm(x, g):
    xf = x.astype(jnp.float32)
    y = xf * lax.rsqrt(jnp.mean(xf * xf, axis=-1, keepdims=True) + EPS)
    return (y * g.astype(jnp.float32)).astype(x.dtype)


def swiglu(h, w_in, w_out):
    a, b = jnp.split(h @ w_in, 2, axis=-1)
    return (jax.nn.silu(a) * b) @ w_out


def memory_attention(mq, mem_n, w_kv, g_q, g_k):
    b, s, _ = mq.shape
    q = rms_norm(mq.reshape(b, s, MEM_HEADS, HEAD_DIM), g_q)
    kv = mem_n @ w_kv
    k, v = jnp.split(kv, 2, axis=-1)
    k = rms_norm(k.reshape(b, MEM_LEN, MEM_HEADS, HEAD_DIM), g_k)
    v = v.reshape(b, MEM_LEN, MEM_HEADS, HEAD_DIM)
    logits = jnp.einsum('bshd,bmhd->bhsm', q, k).astype(jnp.float32) / math.sqrt(HEAD_DIM)
    p = jax.nn.softmax(logits, axis=-1).astype(v.dtype)
    o = jnp.einsum('bhsm,bmhd->bshd', p, v)
    return o.reshape(b, s, MEM_WIDTH)


def forgetting_attention(q, k, v, c):
    b, s, h, d = q.shape
    nblk = s // Q_BLOCK
    qb = q.reshape(b, nblk, Q_BLOCK, h, d).transpose(1, 0, 2, 3, 4)
    cq = c.reshape(b, h, nblk, Q_BLOCK).transpose(2, 0, 1, 3)
    starts = jnp.arange(nblk, dtype=jnp.int32) * Q_BLOCK
    key_pos = jnp.arange(s, dtype=jnp.int32)
    scale = 1.0 / math.sqrt(d)

    def block(args):
        q_i, c_i, start = args
        logits = jnp.einsum('bqhd,bkhd->bhqk', q_i, k).astype(jnp.float32) * scale
        logits = logits + c_i[:, :, :, None] - c[:, :, None, :]
        q_pos = start + jnp.arange(Q_BLOCK, dtype=jnp.int32)
        mask = key_pos[None, :] <= q_pos[:, None]
        logits = jnp.where(mask[None, None], logits, -jnp.inf)
        p = jax.nn.softmax(logits, axis=-1).astype(v.dtype)
        return jnp.einsum('bhqk,bkhd->bqhd', p, v)

    out = lax.map(block, (qb, cq, starts))
    return out.transpose(1, 0, 2, 3, 4).reshape(b, s, h * d)


def fox_token_mixer(proj, b_f, g_q, g_k):
    b, s, _ = proj.shape
    t = TOK_WIDTH
    q = rms_norm(proj[..., :t].reshape(b, s, FOX_HEADS, HEAD_DIM), g_q)
    k = rms_norm(proj[..., t:2 * t].reshape(b, s, FOX_HEADS, HEAD_DIM), g_k)
    v = proj[..., 2 * t:3 * t].reshape(b, s, FOX_HEADS, HEAD_DIM)
    f_logit = proj[..., 3 * t:3 * t + FOX_HEADS].astype(jnp.float32) + b_f.astype(jnp.float32)
    log_f = jax.nn.log_sigmoid(f_logit)
    c = jnp.cumsum(log_f, axis=1).transpose(0, 2, 1)
    mq = proj[..., 3 * t + FOX_HEADS:]
    return forgetting_attention(q, k, v, c), mq


def gmlp_token_mixer(proj, v_gain, w_s, b_s):
    b, s, _ = proj.shape
    t = TOK_WIDTH
    z = jax.nn.gelu(proj[..., :2 * t])
    u, v = z[..., :t], z[..., t:]
    v = rms_norm(v.reshape(b, s, GMLP_GROUPS, HEAD_DIM), v_gain.reshape(GMLP_GROUPS, HEAD_DIM))
    n_chunk = s // CHUNK
    vc = v.reshape(b, n_chunk, CHUNK, GMLP_GROUPS, HEAD_DIM)
    w = jnp.tril(w_s)
    gate = jnp.einsum('gts,bcsgd->bctgd', w, vc) + b_s.T[None, None, :, :, None]
    out = u.reshape(b, n_chunk, CHUNK, GMLP_GROUPS, HEAD_DIM) * gate
    return out.reshape(b, s, t), proj[..., 2 * t:]


def setup_inputs(seed: int = 0) -> dict:
    key = jax.random.key(seed)
    ks = jax.random.split(key, 32)
    f32 = jnp.float32
    D, F = D_MODEL, D_FF

    def nrm(k, shape, scale):
        return jax.random.normal(k, shape, f32) * scale

    def gain(k, shape):
        return 1.0 + 0.02 * jax.random.normal(k, shape, f32)

    return {
        "x": jax.random.normal(ks[0], (BATCH, SEQ, D), f32),
        "mem": jax.random.normal(ks[1], (BATCH, MEM_LEN, D), f32),
        "norm_ffn1": gain(ks[2], (DEPTH, D)),
        "ffn1_w_in": nrm(ks[3], (DEPTH, D, 2 * F), D ** -0.5),
        "ffn1_w_out": nrm(ks[4], (DEPTH, F, D), F ** -0.5),
        "norm_mix": gain(ks[5], (DEPTH, D)),
        "norm_ffn2": gain(ks[6], (DEPTH, D)),
        "ffn2_w_in": nrm(ks[7], (DEPTH, D, 2 * F), D ** -0.5),
        "ffn2_w_out": nrm(ks[8], (DEPTH, F, D), F ** -0.5),
        "w_out": nrm(ks[9], (DEPTH, MIX_WIDTH, D), MIX_WIDTH ** -0.5),
        "mem_norm": gain(ks[10], (D,)),
        "mem_w_kv": nrm(ks[11], (DEPTH, D, 2 * MEM_WIDTH), D ** -0.5),
        "mem_q_norm": gain(ks[12], (DEPTH, HEAD_DIM)),
        "mem_k_norm": gain(ks[13], (DEPTH, HEAD_DIM)),
        "fox_w_in": nrm(ks[14], (N_FOX, D, FOX_IN), D ** -0.5),
        "fox_b_f": 2.0 + 4.0 * jax.random.uniform(ks[15], (N_FOX, FOX_HEADS), f32),
        "fox_q_norm": gain(ks[16], (N_FOX, HEAD_DIM)),
        "fox_k_norm": gain(ks[17], (N_FOX, HEAD_DIM)),
        "gmlp_w_in": nrm(ks[18], (N_GMLP, D, GMLP_IN), D ** -0.5),
        "gmlp_v_norm": gain(ks[19], (N_GMLP, TOK_WIDTH)),
        "gmlp_w_s": nrm(ks[20], (N_GMLP, GMLP_GROUPS, CHUNK, CHUNK), 0.5 * CHUNK ** -0.5),
        "gmlp_b_s": 1.0 + 0.02 * jax.random.normal(ks[21], (N_GMLP, GMLP_GROUPS, CHUNK), f32),
    }


def reference(x, mem, norm_ffn1, ffn1_w_in, ffn1_w_out, norm_mix, norm_ffn2, ffn2_w_in,
              ffn2_w_out, w_out, mem_norm, mem_w_kv, mem_q_norm, mem_k_norm, fox_w_in,
              fox_b_f, fox_q_norm, fox_k_norm, gmlp_w_in, gmlp_v_norm, gmlp_w_s, gmlp_b_s):
    mem_n = rms_norm(mem, mem_norm)
    for i in range(DEPTH):
        kind, j = i % N_MIXERS, i // N_MIXERS
        x = x + 0.5 * swiglu(rms_norm(x, norm_ffn1[i]), ffn1_w_in[i], ffn1_w_out[i])
        h = rms_norm(x, norm_mix[i])
        if kind == 0:
            tok, mq = fox_token_mixer(h @ fox_w_in[j], fox_b_f[j], fox_q_norm[j], fox_k_norm[j])
        else:
            tok, mq = gmlp_token_mixer(h @ gmlp_w_in[j], gmlp_v_norm[j], gmlp_w_s[j], gmlp_b_s[j])
        mo = memory_attention(mq, mem_n, mem_w_kv[i], mem_q_norm[i], mem_k_norm[i])
        x = x + jnp.concatenate([tok, mo], axis=-1) @ w_out[i]
        x = x + 0.5 * swiglu(rms_norm(x, norm_ffn2[i]), ffn2_w_in[i], ffn2_w_out[i])
    return x
```

```python
import numpy as np
import ml_dtypes
from contextlib import ExitStack
import concourse.bass as bass
import concourse.mybir as mybir
from concourse.bass_utils import run_bass_kernel_spmd

F32 = mybir.dt.float32
BF16 = mybir.dt.bfloat16
AF = mybir.ActivationFunctionType
ALU = mybir.AluOpType
AX = mybir.AxisListType

D = 1024
KC = 8
FF = 2816
FC = 22
T = 512
NT = 8
NBLK = 32
EPS = 1e-6
NEG = -30000.0

ENGS = ("pe", "act", "dve", "pool", "sp")


class Buf:
    __slots__ = ("name", "w", "r")

    def __init__(self, name):
        self.name = name
        self.w = {}
        self.r = {}


class Op:
    __slots__ = ("eng", "emit", "deps", "signal", "sigval", "stream", "dval", "key")


class Stream:
    def __init__(self, sem):
        self.sem = sem
        self.count = 0
        self.last = None


class Sched:
    def __init__(self, nc, es):
        self.nc = nc
        self.es = es
        self.ops = {e: [] for e in ENGS}
        self.sem = {e: es.enter_context(nc.semaphore("sem_" + e)) for e in ENGS}
        self.streams = []
        self.nsem = 0

    def ring(self, name, n):
        out = []
        for i in range(n):
            s = Stream(self.es.enter_context(self.nc.semaphore("dq_%s_%d" % (name, i))))
            self.streams.append(s)
            out.append(s)
        return Ring(out)

    def add(self, eng, emit, reads=(), writes=(), stream=None):
        op = Op()
        op.eng = eng
        op.emit = emit
        op.deps = []
        op.signal = False
        op.sigval = 0
        op.stream = stream
        op.dval = 0
        op.key = eng if stream is None else ("dma", id(stream))
        for b in reads:
            op.deps.extend(b.w.values())
        for b in writes:
            op.deps.extend(b.w.values())
            op.deps.extend(b.r.values())
        if stream is not None:
            if stream.last is not None:
                op.deps.append(stream.last)
            stream.count += 1
            op.dval = 16 * stream.count
            stream.last = op
        for b in reads:
            b.r[op.key] = op
        for b in writes:
            b.w = {op.key: op}
            b.r = {}
        self.ops[eng].append(op)
        return op

    def barrier(self):
        lasts = [self.ops[e][-1] for e in ENGS if self.ops[e]]
        lasts = [o for o in lasts if o.emit is not None]
        real = []
        for e in ENGS:
            for o in reversed(self.ops[e]):
                if o.emit is not None and o.stream is None:
                    real.append(o)
                    break
        dl = [s.last for s in self.streams if s.last is not None]
        for e in ENGS:
            op = Op()
            op.eng = e
            op.emit = None
            op.deps = list(real) + list(dl)
            op.signal = False
            op.sigval = 0
            op.stream = None
            op.dval = 0
            op.key = e
            self.ops[e].append(op)

    def finalize(self):
        for e in ENGS:
            for op in self.ops[e]:
                for d in op.deps:
                    if d.stream is None:
                        if d.eng == "pe" and op.eng == "pe" and op.stream is None:
                            continue
                        d.signal = True
        for e in ENGS:
            c = 0
            for op in self.ops[e]:
                if op.signal:
                    c += 1
                    op.sigval = c
            assert c < 60000, (e, c)
        for s in self.streams:
            assert 16 * s.count < 60000, s.count

    def emit_engine(self, e, eng):
        known = {}
        for op in self.ops[e]:
            need = {}
            for d in op.deps:
                if d.stream is None:
                    if d.eng == "pe" and e == "pe" and op.stream is None:
                        continue
                    sem, val = self.sem[d.eng], d.sigval
                else:
                    sem, val = d.stream.sem, d.dval
                k = id(sem)
                if val > need.get(k, (None, 0))[1]:
                    need[k] = (sem, val)
            for k, (sem, val) in need.items():
                if known.get(k, 0) >= val:
                    continue
                eng.wait_ge(sem, val)
                known[k] = val
            if op.emit is None:
                continue
            ins = op.emit(eng)
            if op.stream is not None:
                ins.then_inc(op.stream.sem, 16)
            elif op.signal:
                ins.then_inc(self.sem[e], 1)


class Ring:
    def __init__(self, items):
        self.items = items
        self.i = 0

    def next(self):
        x = self.items[self.i % len(self.items)]
        self.i += 1
        return x


def build(debug=False, upto=99):
    nc = bass.Bass("TRN2", target_bir_lowering=False)
    es = ExitStack()
    S = Sched(nc, es)

    def din(name, shape, dt=F32):
        return nc.dram_tensor(name, list(shape), dt, kind="ExternalInput").ap()

    def dscr(name, shape, dt, dump=False):
        kind = "ExternalOutput" if (debug and dump) else "Internal"
        return nc.dram_tensor(name, list(shape), dt, kind=kind).ap()

    x_own = din("x_own", [NT, 128, KC, T])
    x_oth = din("x_oth", [NT, 128, KC, T])
    mem_in = din("mem", [256, D])
    w_ffn_in = {(i, L): din("ffn%d_w_in_%d" % (i, L), [D, 2 * FF]) for i in (1, 2) for L in (0, 1)}
    w_ffn_out = {(i, L): din("ffn%d_w_out_%d" % (i, L), [FF, D]) for i in (1, 2) for L in (0, 1)}
    w_o = {L: din("w_out_%d" % L, [D, D]) for L in (0, 1)}
    w_memkv = {L: din("mem_w_kv_%d" % L, [D, 512]) for L in (0, 1)}
    w_fox = din("fox_w_in", [D, 2572])
    w_gmlp = din("gmlp_w_in", [D, 1792])
    gains_d = din("gains", [128, 48])
    hcols_d = din("hcols", [128, 8])
    hrows_d = din("hrows", [128, 128])
    bf_d = din("bf_b", [128, 12])
    memnorm_d = din("memnorm_b", [128, D])
    vgain_d = din("vgain_c", [128, 6])
    ws_d = din("gmlp_w_s", [12, 128, 128])
    bs_d = din("bs_pair", [128, 6, 128])
    ident_d = din("ident", [128, 128])
    tri_d = din("tri01", [128, 128])
    trineg_d = din("trineg", [128, 128])
    othmask_d = din("othmask", [128, 128])
    bdiag_d = din("bdiag", [128, 128])
    pflag_d = din("pflag", [128, 2])
    out_d = nc.dram_tensor("out", [NT, 128, KC, T], F32, kind="ExternalOutput").ap()

    wb_ffn_in = {k: dscr("b_ffn%d_in_%d" % k, [D, 2 * FF], BF16) for k in w_ffn_in}
    wb_ffn_out = {k: dscr("b_ffn%d_out_%d" % k, [FF, D], BF16) for k in w_ffn_out}
    wb_o = {L: dscr("b_wo_%d" % L, [D, D], BF16) for L in (0, 1)}
    wb_memkv = {L: dscr("b_memkv_%d" % L, [D, 512], BF16) for L in (0, 1)}
    wb_fox = dscr("b_fox", [D, 2572], BF16)
    wb_gmlp = dscr("b_gmlp", [D, 1792], BF16)
    X = dscr("X", [NT, 128, KC, T], F32, dump=True)
    Ks = dscr("Ks", [12, 64, 8192], BF16, dump=True)
    Vs = dscr("Vs", [12, 128, 64, 65], BF16, dump=True)
    Qs = dscr("Qs", [12, 65, 4096], BF16, dump=True)
    CATs = dscr("CATs", [16, 64, 4096], BF16, dump=True)
    dbg_bias = dscr("dbg_bias", [128, 768], F32, dump=True) if debug else None

    dram_bufs = {}

    def dbuf(name):
        if name not in dram_bufs:
            dram_bufs[name] = Buf(name)
        return dram_bufs[name]

    ARENA = 211968
    PBASE = 24 * 1024
    arena = es.enter_context(nc.sbuf_tensor("arena", [128, ARENA], mybir.dt.uint8))
    tops = {"p": 0, "f": PBASE}
    isz = {F32: 4, BF16: 2}

    def phase_reset():
        tops["f"] = PBASE

    def sb(stack, name, shape, dt):
        n = isz[dt]
        for d_ in shape[1:]:
            n *= d_
        n_exact = n
        n = (n + 31) // 32 * 32
        which = "p" if stack is es else "f"
        off = tops[which]
        tops[which] = off + n
        n = n_exact
        assert tops["p"] <= PBASE, (name, tops)
        assert tops["f"] <= ARENA, (name, tops)
        ap = arena[0:shape[0], off:off + n].bitcast(dt)
        if len(shape) == 3:
            ap = ap.rearrange("p (a b) -> p a b", a=shape[1])
        elif len(shape) == 4:
            ap = ap.rearrange("p (a b c) -> p a b c", a=shape[1], b=shape[2])
        return ap, Buf(name)

    class _Rec:
        def __getattr__(self, name):
            def f(*a, **kw):
                self.call = (name, a, kw)
            return f

    banks = []
    for i in range(8):
        t = es.enter_context(nc.psum_tensor("bank%d" % i, [128, 512], F32))
        banks.append((t, Buf("bank%d" % i)))
    bank_ring = Ring(banks)

    def dma(ring, out, in_, reads, writes, eng=None):
        if eng is None:
            eng = "pool" if (ring is st_ring or ring is stb_ring) else "sp"
        st = ring.next()
        return S.add(eng, lambda e: e.dma_start(out=out, in_=in_), reads=reads, writes=writes, stream=st)

    def mm(out, lhsT, rhs, start, stop, reads, writes):
        return S.add("pe", lambda e: e.matmul(out, lhsT, rhs, start=start, stop=stop), reads=reads, writes=writes)

    def act(out, in_, func, reads, writes, bias=None, scale=None):
        kw = {}
        if bias is not None:
            kw["bias"] = bias
        if scale is not None:
            kw["scale"] = scale
        return S.add("act", lambda e: e.activation(out=out, in_=in_, func=func, **kw), reads=reads, writes=writes)

    def vec(eng, fn, reads, writes):
        r = _Rec()
        fn(r)
        name, a, kw = r.call
        return S.add(eng, lambda e: getattr(e, name)(*a, **kw), reads=reads, writes=writes)

    ld_ring = S.ring("ld", 8)
    cast_ring = S.ring("cast", 8)
    st_ring = S.ring("st", 8)
    stb_ring = S.ring("stb", 4)

    ident_f, B_ident_f = sb(es, "ident_f", [128, 128], F32)
    ident_b, B_ident_b = sb(es, "ident_b", [128, 128], BF16)
    ones_b, B_ones_b = sb(es, "ones_b", [128, 128], BF16)
    ones_f, B_ones_f = sb(es, "ones_f", [128, 128], F32)
    bdiag_b, B_bdiag_b = sb(es, "bdiag_b", [128, 128], BF16)
    tri_f, B_tri_f = sb(es, "tri_f", [128, 128], F32)
    trineg_b, B_trineg_b = sb(es, "trineg_b", [128, 128], BF16)
    othmask_b, B_othmask_b = sb(es, "othmask_b", [128, 128], BF16)
    gains, B_gains = sb(es, "gains", [128, 48], F32)
    hcols, B_hcols = sb(es, "hcols", [128, 8], F32)
    hcols8, B_hcols8 = sb(es, "hcols8", [128, 8], F32)
    pflag, B_pflag = sb(es, "pflag", [128, 2], F32)
    tmpc, B_tmpc = sb(es, "tmpc", [128, 128], F32)
    kmT = {}
    VmP = {}
    for L in (0, 1):
        kmT[L] = sb(es, "kmT%d" % L, [128, 2, 256], BF16)
        VmP[L] = sb(es, "VmP%d" % L, [128, 2, 4, 65], BF16)
    Mcol, B_Mcol = sb(es, "Mcol", [128, 4], F32)

    dma(ld_ring, ident_f[:], ident_d[:, :], [], [B_ident_f])
    dma(ld_ring, tri_f[:], tri_d[:, :], [], [B_tri_f])
    dma(ld_ring, gains[:], gains_d[:, :], [], [B_gains])
    dma(ld_ring, hcols[:], hcols_d[:, :], [], [B_hcols])
    dma(ld_ring, pflag[:], pflag_d[:, :], [], [B_pflag])
    vec("dve", lambda e: e.tensor_copy(out=ident_b[:], in_=ident_f[:]), [B_ident_f], [B_ident_b])
    vec("dve", lambda e: e.memset(ones_b[:], 1.0), [], [B_ones_b])
    vec("dve", lambda e: e.memset(ones_f[:], 1.0), [], [B_ones_f])
    dma(ld_ring, tmpc[:], bdiag_d[:, :], [], [B_tmpc])
    vec("dve", lambda e: e.tensor_copy(out=bdiag_b[:], in_=tmpc[:]), [B_tmpc], [B_bdiag_b])
    dma(ld_ring, tmpc[:], trineg_d[:, :], [], [B_tmpc])
    vec("dve", lambda e: e.tensor_copy(out=trineg_b[:], in_=tmpc[:]), [B_tmpc], [B_trineg_b])
    dma(ld_ring, tmpc[:], othmask_d[:, :], [], [B_tmpc])
    vec("dve", lambda e: e.tensor_copy(out=othmask_b[:], in_=tmpc[:]), [B_tmpc], [B_othmask_b])
    vec("dve", lambda e: e.tensor_scalar_mul(out=hcols8[:], in0=hcols[:], scalar1=0.125), [B_hcols], [B_hcols8])
    dma(ld_ring, tmpc[:], hrows_d[:, :], [], [B_tmpc])
    mx, B_mx = sb(es, "mx", [128, 8], F32)
    for i in range(2):
        vec("dve", (lambda i: lambda e: e.reduce_max(out=mx[:, i:i + 1], in_=tmpc[:, 64 * i:64 * i + 64],
                                                     axis=AX.X, apply_absolute_value=True))(i), [B_tmpc], [B_mx])
    hrows2_d = din("hrows2", [128, 256])
    tmpc2, B_tmpc2 = sb(es, "tmpc2", [128, 256], F32)
    dma(ld_ring, tmpc2[:], hrows2_d[:, :], [], [B_tmpc2])
    for i in range(4):
        vec("dve", (lambda i: lambda e: e.reduce_max(out=mx[:, 2 + i:3 + i], in_=tmpc2[:, 64 * i:64 * i + 64],
                                                     axis=AX.X, apply_absolute_value=True))(i), [B_tmpc2], [B_mx])
    for j in range(3):
        vec("dve", (lambda j: lambda e: e.tensor_tensor(out=Mcol[:, j:j + 1], in0=mx[:, 2 * j:2 * j + 1], in1=mx[:, 2 * j + 1:2 * j + 2], op=ALU.mult))(j),
            [B_mx], [B_Mcol])
    vec("dve", lambda e: e.tensor_scalar_mul(out=Mcol[:, 0:3], in0=Mcol[:, 0:3], scalar1=-8.0), [B_Mcol], [B_Mcol])

    cast_gate = [[]]

    def cast_cols(dst, src, bname, c0, ncols):
        dma(cast_ring, dst[:, c0:c0 + ncols], src[:, c0:c0 + ncols], list(cast_gate[0]), [dbuf(bname)], eng="pool")

    FOX_G = ((768, 512), (1280, 256), (0, 512), (512, 256), (1536, 512), (2048, 268), (2316, 256))
    GMLP_G = ((0, 512), (512, 256), (768, 512), (1280, 256), (1536, 256))

    def cast_ffn(key):
        for g in range(6):
            nc_ = 512 if g < 5 else 256
            cast_cols(wb_ffn_in[key], w_ffn_in[key], "wfi%d%d_a%d" % (key + (g,)), 512 * g, nc_)
            cast_cols(wb_ffn_in[key], w_ffn_in[key], "wfi%d%d_b%d" % (key + (g,)), FF + 512 * g, nc_)
        for dp in range(4):
            cast_cols(wb_ffn_out[key], w_ffn_out[key], "wfo%d%d_%d" % (key + (dp,)), 256 * dp, 256)

    cast_cols(wb_memkv[0], w_memkv[0], "wmemkv0", 0, 512)
    cast_cols(wb_memkv[1], w_memkv[1], "wmemkv1", 0, 512)
    cast_ffn((1, 0))
    for c0, ncl in FOX_G:
        cast_cols(wb_fox, w_fox, "wfox_%d" % c0, c0, ncl)

    def late_casts(gate):
        cast_gate[0] = gate
        for dp in range(4):
            cast_cols(wb_o[0], w_o[0], "wo0_%d" % dp, 256 * dp, 256)
        cast_ffn((2, 0))
        cast_ffn((1, 1))
        for c0, ncl in GMLP_G:
            cast_cols(wb_gmlp, w_gmlp, "wgmlp_%d" % c0, c0, ncl)
        for dp in range(4):
            cast_cols(wb_o[1], w_o[1], "wo1_%d" % dp, 256 * dp, 256)
        cast_ffn((2, 1))
        cast_gate[0] = []

    W = {}

    def alloc_work(stack, nwk=4, share=False):
        W["xT2"] = [sb(stack, "xT%d" % i, [128, KC, T], F32) for i in range(2)]
        W["hT"] = sb(stack, "hT", [128, KC, T], BF16)
        W["hT2"] = sb(stack, "hT2", [128, KC, T], BF16)
        W["sq"] = sb(stack, "sq", [128, KC, T], BF16)
        W["lnt"] = sb(stack, "lnt", [128, T], F32)
        W["rstd"] = sb(stack, "rstd", [128, T], F32)
        W["sq2"] = sb(stack, "sq2", [128, T], BF16)
        W["lnt2"] = W["lnt"] if share else sb(stack, "lnt2", [128, T], F32)
        W["rstd2"] = W["rstd"] if share else sb(stack, "rstd2", [128, T], F32)
        W["gT"] = sb(stack, "gT", [128, FC, T], BF16)
        W["s_ring"] = Ring([sb(stack, "silu%d" % i, [128, T], F32) for i in range(2)])
        W["wk_ring"] = Ring([sb(stack, "wk%d" % i, [128, KC, 512], BF16) for i in range(nwk)])
        W["wo_ring"] = Ring([sb(stack, "wo%d" % i, [128, FC, 256], BF16) for i in range(2)])

    wk_dma = S.ring("wk", 4)
    wo_dma = S.ring("wo", 2)

    def load_wk(wsrc, bname, c0, ncols):
        t, b = W["wk_ring"].next()
        src = wsrc[:, c0:c0 + ncols].rearrange("(k p) c -> p k c", p=128)
        dma(wk_dma, t[:, :, 0:ncols], src, [dbuf(bname)], [b])
        return t, b

    def norm_a(xTt):
        xT, B_xT = xTt
        sq, B_sq = W["sq"]
        act(sq[:], xT[:], AF.Square, [B_xT], [B_sq])

    def norm_b(xTt, gcol0, hTt):
        xT, B_xT = xTt
        hT, B_hT = hTt
        sq, B_sq = W["sq"]
        lnt, B_lnt = W["lnt"]
        rstd, B_rstd = W["rstd"]
        pt, pb = bank_ring.next()
        for k in range(KC):
            mm(pt[:, :], ones_b[:], sq[:, k, :], k == 0, k == KC - 1, [B_ones_b, B_sq], [pb])
        act(lnt[:], pt[:, :], AF.Ln, [pb], [B_lnt], bias=EPS, scale=1.0 / D)
        act(rstd[:], lnt[:], AF.Exp, [B_lnt], [B_rstd], scale=-0.5)
        for k in range(KC):
            vec("dve", (lambda k: lambda e: e.scalar_tensor_tensor(out=hT[:, k, :], in0=xT[:, k, :], scalar=gains[:, gcol0 + k:gcol0 + k + 1],
                                                                in1=rstd[:], op0=ALU.mult, op1=ALU.mult))(k),
                [B_xT, B_gains, B_rstd], [B_hT])

    def ffn_ab(key, hook=None):
        hT, B_hT = W["hT"]
        gT, B_gT = W["gT"]
        win = wb_ffn_in[key]
        for g in range(6):
            nf = 4 if g < 5 else 2
            ta, ba = load_wk(win, "wfi%d%d_a%d" % (key + (g,)), 512 * g, 128 * nf)
            tb, bb = load_wk(win, "wfi%d%d_b%d" % (key + (g,)), FF + 512 * g, 128 * nf)
            for fi in range(nf):
                f = 4 * g + fi
                pa, pba = bank_ring.next()
                pbt, pbb = bank_ring.next()
                for k in range(KC):
                    mm(pa[:, :], ta[:, k, 128 * fi:128 * fi + 128], hT[:, k, :], k == 0, k == KC - 1, [ba, B_hT], [pba])
                for k in range(KC):
                    mm(pbt[:, :], tb[:, k, 128 * fi:128 * fi + 128], hT[:, k, :], k == 0, k == KC - 1, [bb, B_hT], [pbb])
                st, sbf = W["s_ring"].next()
                act(st[:], pa[:, :], AF.Silu, [pba], [sbf])
                vec("dve", (lambda f, st, pbt: lambda e: e.tensor_tensor(out=gT[:, f, :], in0=pbt[:, :], in1=st[:], op=ALU.mult))(f, st, pbt),
                    [pbb, sbf], [B_gT])
            if g == 1 and hook is not None:
                hook()

    def ffn_y_load(key, dp):
        t, b = W["wo_ring"].next()
        src = wb_ffn_out[key][:, 256 * dp:256 * dp + 256].rearrange("(f p) c -> p f c", p=128)
        dma(wo_dma, t[:, :, :], src, [dbuf("wfo%d%d_%d" % (key + (dp,)))], [b])
        return t, b

    def ffn_y_prefetch(key):
        return [ffn_y_load(key, 0), ffn_y_load(key, 1)]

    def ffn_y(key, xTt, hook=None, pre=None):
        xT, B_xT = xTt
        gT, B_gT = W["gT"]
        pre = list(pre) if pre else []
        for dp in range(4):
            t, b = pre.pop(0) if pre else ffn_y_load(key, dp)
            for dd in range(2):
                do = 2 * dp + dd
                py, pyb = bank_ring.next()
                for f in range(FC):
                    mm(py[:, :], t[:, f, 128 * dd:128 * dd + 128], gT[:, f, :], f == 0, f == FC - 1, [b, B_gT], [pyb])
                vec("dve", (lambda do, py: lambda e: e.scalar_tensor_tensor(out=xT[:, do, :], in0=py[:, :], scalar=0.5, in1=xT[:, do, :],
                                                                           op0=ALU.mult, op1=ALU.add))(do, py),
                    [pyb, B_xT], [B_xT])
            if dp == 1 and hook is not None:
                hook()

    def pair_norm(pt, pb, gcol, out_ap, out_buf, n=T):
        sq2, B_sq2 = W["sq2"]
        lnt2, B_lnt2 = W["lnt2"]
        rstd2, B_rstd2 = W["rstd2"]
        act(sq2[:, 0:n], pt[:, 0:n], AF.Square, [pb], [B_sq2])
        p2, p2b = bank_ring.next()
        mm(p2[:, 0:n], bdiag_b[:], sq2[:, 0:n], True, True, [B_bdiag_b, B_sq2], [p2b])
        act(lnt2[:, 0:n], p2[:, 0:n], AF.Ln, [p2b], [B_lnt2], bias=EPS, scale=1.0 / 64)
        act(rstd2[:, 0:n], lnt2[:, 0:n], AF.Exp, [B_lnt2], [B_rstd2], scale=-0.5)
        vec("dve", lambda e: e.scalar_tensor_tensor(out=out_ap, in0=pt[:, 0:n], scalar=gcol, in1=rstd2[:, 0:n], op0=ALU.mult, op1=ALU.mult),
            [pb, B_rstd2, B_hcols, B_hcols8], [out_buf])

    def run_tiles(n, pre_a, pre_b, ab, y, mid_a, mid_b, post, key):
        pre_a(0)
        pre_b(0)
        ab(0, None)
        keyf = key if callable(key) else (lambda j_: key)
        yp = ffn_y_prefetch(keyf(0))
        for j in range(n):
            if j + 1 < n:
                pre_a(j + 1)
            y(j, (lambda j=j: pre_b(j + 1)) if j + 1 < n else None, yp)
            mid_a(j)
            if j + 1 < n:
                ab(j + 1, (lambda j=j: mid_b(j)))
                yp = ffn_y_prefetch(keyf(j + 1))
            else:
                mid_b(j)
            post(j)

    def nop(*a):
        return None

    pT_ring = Ring([sb(es, "pT%d" % i, [128, T], BF16) for i in range(4)])
    o_sb, B_o_sb = sb(es, "o_sb", [65, T], F32)
    rec, B_rec = sb(es, "rec", [65, T], F32)

    def normalize_out(po, pob, out_ap, out_buf, bank=None):
        vec("dve", lambda e: e.reciprocal(out=rec[64:65, :], in_=po[64:65, :]), [pob], [B_rec])
        act(o_sb[0:64, :], po[0:64, :], AF.Copy, [pob, B_rec], [B_o_sb])
        pc, pcb = bank_ring.next() if bank is None else bank
        mm(pc[0:64, :], ones_f[64:65, 0:64], rec[64:65, :], True, True, [B_ones_f, B_rec], [pcb])
        vec("dve", lambda e: e.tensor_tensor(out=out_ap, in0=pc[0:64, :], in1=o_sb[0:64, :], op=ALU.mult), [pcb, B_o_sb], [out_buf])

    def mem_attn(L, mqn, B_mqn, out_fn, after=None):
        kt, kb = kmT[L]
        vt, vb = VmP[L]

        def s_stage(hh):
            pr, lo = hh // 2, 64 * (hh % 2)
            out = []
            for mb in range(2):
                ps, psb = banks[4 + 2 * (hh % 2) + mb]
                mm(ps[:, :], kt[lo:lo + 64, pr, 128 * mb:128 * mb + 128], mqn[lo:lo + 64, pr, :], True, True, [kb, B_mqn], [psb])
                out.append((ps, psb))
            return out

        nxt = s_stage(0)
        done = []
        for hh in range(4):
            curs = nxt
            po, pob = banks[hh]
            pts = []
            for mb in range(2):
                ps, psb = curs[mb]
                pt_, ptb = pT_ring.next()
                act(pt_[:], ps[:, :], AF.Exp, [psb, B_Mcol], [ptb], bias=Mcol[:, 1 + L:2 + L])
                pts.append((pt_, ptb))
            if hh + 1 < 4:
                nxt = s_stage(hh + 1)
            for mb in range(2):
                pt_, ptb = pts[mb]
                mm(po[0:65, :], vt[:, mb, hh, :], pt_[:], mb == 0, mb == 1, [vb, ptb], [pob])
            done.append((hh, po, pob))
        for i_, (hh, po, pob) in enumerate(done):
            oap, obuf = out_fn(hh)
            normalize_out(po, pob, oap, obuf, bank=banks[4 + i_])
            if after is not None:
                after(hh, oap, obuf)

    with ExitStack() as ps0:
        phase_reset()
        alloc_work(ps0)
        memt, B_memt = sb(ps0, "memt", [128, 2, D], F32)
        memsq, B_memsq = sb(ps0, "memsq", [128, D], F32)
        mss, B_mss = sb(ps0, "mss", [128, 2], F32)
        mnb, B_mnb = sb(ps0, "mnb", [128, D], F32)
        memn, B_memn = sb(ps0, "memn", [128, 2, D], F32)
        memnT, B_memnT = sb(ps0, "memnT", [128, KC, 256], BF16)
        wkv, B_wkv = sb(ps0, "wkv", [128, KC, 512], BF16)
        dma(ld_ring, mnb[:], memnorm_d[:, :], [], [B_mnb])
        for mb in range(2):
            dma(ld_ring, memt[:, mb, :], mem_in[128 * mb:128 * mb + 128, :], [], [B_memt])
        for mb in range(2):
            act(memsq[:], memt[:, mb, :], AF.Square, [B_memt], [B_memsq])
            vec("dve", (lambda mb: lambda e: e.reduce_sum(out=mss[:, mb:mb + 1], in_=memsq[:], axis=AX.X))(mb), [B_memsq], [B_mss])
        act(mss[:], mss[:], AF.Ln, [B_mss], [B_mss], bias=EPS, scale=1.0 / D)
        act(mss[:], mss[:], AF.Exp, [B_mss], [B_mss], scale=-0.5)
        for mb in range(2):
            vec("dve", (lambda mb: lambda e: e.scalar_tensor_tensor(out=memn[:, mb, :], in0=memt[:, mb, :], scalar=mss[:, mb:mb + 1], in1=mnb[:],
                                                                  op0=ALU.mult, op1=ALU.mult))(mb), [B_memt, B_mss, B_mnb], [B_memn])
        for mb in range(2):
            for kq in range(2):
                pt, pb = bank_ring.next()
                for kk in range(4):
                    k = 4 * kq + kk
                    vec("pe", (lambda pt, kk, mb, k: lambda e: e.transpose(out=pt[:, 128 * kk:128 * kk + 128], in_=memn[:, mb, 128 * k:128 * k + 128],
                                                                             identity=ident_f[:]))(pt, kk, mb, k),
                          [B_memn, B_ident_f], [pb])
                vec("dve", (lambda pt, kq, mb: lambda e: e.tensor_copy(out=memnT[:, 4 * kq:4 * kq + 4, 128 * mb:128 * mb + 128],
                                                                      in_=pt[:, :].rearrange("p (k t) -> p k t", k=4)))(pt, kq, mb), [pb], [B_memnT])
        for L in (0, 1):
            src = wb_memkv[L][:, :].rearrange("(k p) c -> p k c", p=128)
            dma(ld_ring, wkv[:], src, [dbuf("wmemkv%d" % L)], [B_wkv])
            kt, kb = kmT[L]
            vt, vb = VmP[L]
            vec("dve", (lambda vt: lambda e: e.memset(vt[:], 1.0))(vt), [], [vb])
            for pr in range(2):
                pt, pb = bank_ring.next()
                for k in range(KC):
                    mm(pt[:, 0:256], wkv[:, k, 128 * pr:128 * pr + 128], memnT[:, k, :], k == 0, k == KC - 1, [B_wkv, B_memnT], [pb])
                pair_norm(pt, pb, hcols[:, 3 + 2 * L:4 + 2 * L], kt[:, pr, :], kb, n=256)
            for mb in range(2):
                pt, pb = bank_ring.next()
                for k in range(KC):
                    mm(pt[:, 0:256], memnT[:, k, 128 * mb:128 * mb + 128], wkv[:, k, 256:512], k == 0, k == KC - 1, [B_wkv, B_memnT], [pb])
                vec("dve", (lambda vt, mb, pt: lambda e: e.tensor_copy(out=vt[:, mb, :, 0:64], in_=pt[:, 0:256].rearrange("p (h d) -> p h d", h=4)))(vt, mb, pt),
                    [pb], [vb])
    S.barrier()

    LZ, B_LZ = sb(es, "LZ", [128, 64, 12], F32)
    with ExitStack() as ps1:
        phase_reset()
        alloc_work(ps1)
        kn_ring = Ring([sb(ps1, "kn%d" % i, [128, T], BF16) for i in range(2)])
        vp_ring = Ring([sb(ps1, "vp%d" % i, [128, 12, 4, 65], BF16) for i in range(2)])
        mqn, B_mqn = sb(ps1, "mqn", [128, 2, T], BF16)
        mo_ring = Ring([sb(ps1, "mo%d" % i, [64, T], BF16) for i in range(2)])
        bfb, B_bfb = sb(ps1, "bfb", [128, 12], F32)
        dma(ld_ring, bfb[:], bf_d[:, :], [], [B_bfb])
        for t_, b_ in vp_ring.items:
            vec("dve", (lambda t_: lambda e: e.memset(t_[:], 1.0))(t_), [], [b_])

        def p1_pre_a(jj):
            j, own = jj // 2, jj % 2 == 0
            xT, B_xT = W["xT2"][jj % 2]
            xsrc = x_own if own else x_oth
            dma(ld_ring, xT[:], xsrc[j], [], [B_xT])
            norm_a((xT, B_xT))

        def p1_mid_a(jj):
            j, own = jj // 2, jj % 2 == 0
            xT, B_xT = W["xT2"][jj % 2]
            if own:
                dma(stb_ring, X[j], xT[:], [B_xT], [dbuf("X%d" % j)])
            norm_a((xT, B_xT))

        def p1_post(jj):
            j, own = jj // 2, jj % 2 == 0
            hT, B_hT = W["hT2"]
            slot0 = 4 * j if own else 32 + 4 * j
            kinds = [("k", 768, Ks)] + ([("q", 0, Qs)] if own else [])
            pend = []

            def flush_pair():
                pt, pb, nm, pr = pend.pop(0)
                kt_, kb_ = kn_ring.next()
                gcol = hcols[:, 1:2] if nm == "k" else hcols8[:, 0:1]
                pair_norm(pt, pb, gcol, kt_[:], kb_)
                for hh in range(2):
                    h = 2 * pr + hh
                    if nm == "k":
                        dap = Ks[h, :, 128 * slot0:128 * slot0 + 512]
                        dn = "Ks"
                    else:
                        dap = Qs[h, 0:64, 512 * j:512 * j + 512]
                        dn = "Qs"
                    dma(st_ring, dap, kt_[64 * hh:64 * hh + 64, :], [kb_], [dbuf(dn)])

            for nm, cbase, dst in kinds:
                for (c0, npair, p0) in ((cbase, 4, 0), (cbase + 512, 2, 4)):
                    tw, bw = load_wk(wb_fox, "wfox_%d" % c0, c0, 128 * npair)
                    for pi in range(npair):
                        pr = p0 + pi
                        pt, pb = bank_ring.next()
                        for k in range(KC):
                            mm(pt[:, :], tw[:, k, 128 * pi:128 * pi + 128], hT[:, k, :], k == 0, k == KC - 1, [bw, B_hT], [pb])
                        pend.append((pt, pb, nm, pr))
                        if len(pend) > 1:
                            flush_pair()
            mq_pend = []
            if own:
                tw, bw = load_wk(wb_fox, "wfox_2316", 2316, 256)
                for pr in range(2):
                    pt, pb = bank_ring.next()
                    for k in range(KC):
                        mm(pt[:, :], tw[:, k, 128 * pr:128 * pr + 128], hT[:, k, :], k == 0, k == KC - 1, [bw, B_hT], [pb])
                    mq_pend.append((pt, pb, pr))
                    if pr == 0:
                        flush_pair()
                for (pt, pb, pr) in mq_pend:
                    pair_norm(pt, pb, hcols8[:, 2:3], mqn[:, pr, :], B_mqn)
            else:
                flush_pair()
            tv1, bv1 = load_wk(wb_fox, "wfox_1536", 1536, 512)
            tv2, bv2 = load_wk(wb_fox, "wfox_2048", 2048, 268)
            vt_, vb_ = vp_ring.next()
            for bi in range(4):
                pa, pab = bank_ring.next()
                pb2, pbb = bank_ring.next()
                for k in range(KC):
                    mm(pa[:, :], hT[:, k, 128 * bi:128 * bi + 128], tv1[:, k, 0:512], k == 0, k == KC - 1, [bv1, B_hT], [pab])
                for k in range(KC):
                    mm(pb2[:, 0:268], hT[:, k, 128 * bi:128 * bi + 128], tv2[:, k, 0:268], k == 0, k == KC - 1, [bv2, B_hT], [pbb])
                act(vt_[:, 0:8, bi, 0:64], pa[:, :].rearrange("p (h d) -> p h d", h=8), AF.Copy, [pab], [vb_])
                vec("dve", (lambda vt_, bi, pb2: lambda e: e.tensor_copy(out=vt_[:, 8:12, bi, 0:64],
                                                                        in_=pb2[:, 0:256].rearrange("p (h d) -> p h d", h=4)))(vt_, bi, pb2),
                    [pbb], [vb_])
                vec("dve", (lambda bi, pb2, slot0: lambda e: e.tensor_tensor(out=LZ[:, slot0 + bi, :], in0=pb2[:, 256:268], in1=bfb[:], op=ALU.add))(bi, pb2, slot0),
                    [pbb, B_bfb], [B_LZ])
            dma(stb_ring, Vs[:, :, slot0:slot0 + 4, :].rearrange("h p s c -> p h s c"), vt_[:], [vb_], [dbuf("Vs")])
            if own:
                stores = []

                def out_fn(hh):
                    t_, b_ = mo_ring.next()
                    stores.append((hh, t_, b_))
                    return t_[:], b_
                mem_attn(0, mqn, B_mqn, out_fn, lambda hh, t_, b_: dma(st_ring, CATs[12 + hh, :, 512 * j:512 * j + 512], t_[:], [b_], [dbuf("CATs")]))

        if upto >= 1:
            run_tiles(2 * NT,
                      p1_pre_a,
                      lambda jj: norm_b(W["xT2"][jj % 2], 0, W["hT"]),
                      lambda jj, hook: ffn_ab((1, 0), hook),
                      lambda jj, hook, yp: ffn_y((1, 0), W["xT2"][jj % 2], hook, yp),
                      p1_mid_a,
                      lambda jj: norm_b(W["xT2"][jj % 2], 8, W["hT2"]),
                      p1_post, (1, 0))
    S.barrier()

    biasK, B_biasK = sb(es, "biasK", [128, 64, 12], F32)
    if upto >= 2:
        with ExitStack() as ps2:
            phase_reset()
            nl, B_nl = sb(ps2, "nl", [128, 768], F32)
            tot, B_tot = sb(ps2, "tot", [128, 64, 12], F32)
            pa_, B_pa = sb(ps2, "ppa", [128, 32, 12], F32)
            pb_, B_pb = sb(ps2, "ppb", [128, 32, 12], F32)
            pair, B_pair = sb(ps2, "pair", [128, 32, 12], F32)
            off, B_off = sb(ps2, "off", [128, 64, 12], F32)
            cq, B_cq = sb(ps2, "cq", [128, 12, 32], F32)
            cqr, B_cqr = sb(ps2, "cqr", [128, 3, 128], BF16)
            LZf = LZ[:].rearrange("p s h -> p (s h)")
            act(nl[:], LZf, AF.Exp, [B_LZ], [B_nl], scale=-1.0)
            act(nl[:], nl[:], AF.Ln, [B_nl], [B_nl], bias=1.0)
            pc1, pc1b = bank_ring.next()
            pc2, pc2b = bank_ring.next()
            pt1, pt1b = bank_ring.next()
            pt2, pt2b = bank_ring.next()
            mm(pc1[:, :], tri_f[:], nl[:, 0:512], True, True, [B_tri_f, B_nl], [pc1b])
            mm(pc2[:, 0:256], tri_f[:], nl[:, 512:768], True, True, [B_tri_f, B_nl], [pc2b])
            mm(pt1[:, :], ones_f[:], nl[:, 0:512], True, True, [B_ones_f, B_nl], [pt1b])
            mm(pt2[:, 0:256], ones_f[:], nl[:, 512:768], True, True, [B_ones_f, B_nl], [pt2b])
            totf = tot[:].rearrange("p s h -> p (s h)")
            vec("dve", lambda e: e.tensor_copy(out=totf[:, 0:512], in_=pt1[:, :]), [pt1b], [B_tot])
            vec("dve", lambda e: e.tensor_copy(out=totf[:, 512:768], in_=pt2[:, 0:256]), [pt2b], [B_tot])
            vec("dve", lambda e: e.tensor_tensor(out=pair[:], in0=tot[:, 0:32, :], in1=tot[:, 32:64, :], op=ALU.add), [B_tot], [B_pair])
            vec("dve", lambda e: e.tensor_copy(out=pa_[:], in_=pair[:]), [B_pair], [B_pa])
            cur, curb, nxt, nxtb = pa_, B_pa, pb_, B_pb
            d = 1
            while d < 32:
                vec("dve", (lambda cur, nxt, d: lambda e: e.tensor_copy(out=nxt[:, 0:d, :], in_=cur[:, 0:d, :]))(cur, nxt, d), [curb], [nxtb])
                vec("dve", (lambda cur, nxt, d: lambda e: e.tensor_tensor(out=nxt[:, d:32, :], in0=cur[:, d:32, :], in1=cur[:, 0:32 - d, :], op=ALU.add))(cur, nxt, d),
                    [curb], [nxtb])
                cur, curb, nxt, nxtb = nxt, nxtb, cur, curb
                d *= 2
            vec("dve", (lambda cur: lambda e: e.tensor_tensor(out=pair[:], in0=cur[:], in1=pair[:], op=ALU.subtract))(cur), [curb, B_pair], [B_pair])
            vec("dve", lambda e: e.scalar_tensor_tensor(out=off[:, 0:32, :], in0=tot[:, 32:64, :], scalar=pflag[:, 0:1], in1=pair[:], op0=ALU.mult, op1=ALU.add),
                [B_tot, B_pflag, B_pair], [B_off])
            vec("dve", lambda e: e.scalar_tensor_tensor(out=off[:, 32:64, :], in0=tot[:, 0:32, :], scalar=pflag[:, 1:2], in1=pair[:], op0=ALU.mult, op1=ALU.add),
                [B_tot, B_pflag, B_pair], [B_off])
            offf = off[:].rearrange("p s h -> p (s h)")
            bKf = biasK[:].rearrange("p s h -> p (s h)")
            vec("dve", lambda e: e.tensor_tensor(out=offf[:, 0:512], in0=pc1[:, :], in1=offf[:, 0:512], op=ALU.add), [pc1b, B_off], [B_off])
            vec("dve", lambda e: e.tensor_tensor(out=offf[:, 512:768], in0=pc2[:, 0:256], in1=offf[:, 512:768], op=ALU.add), [pc2b, B_off], [B_off])
            vec("dve", lambda e: e.tensor_scalar(out=bKf, in0=offf, scalar1=Mcol[:, 0:1], scalar2=None, op0=ALU.add), [B_off, B_Mcol], [B_biasK])
            if debug:
                dma(st_ring, dbg_bias[:, :], bKf, [B_biasK], [dbuf("dbg_bias")])
            vec("dve", lambda e: e.tensor_scalar_mul(out=cq[:], in0=off[:, 0:32, :].rearrange("p b h -> p h b"), scalar1=-1.0), [B_off], [B_cq])
            cqf = cq[:].rearrange("p h b -> p (h b)")
            for i in range(3):
                pt, pb = bank_ring.next()
                vec("pe", (lambda pt, i: lambda e: e.transpose(out=pt[:, 0:128], in_=cqf[:, 128 * i:128 * i + 128], identity=ident_f[:]))(pt, i),
                      [B_cq, B_ident_f], [pb])
                vec("dve", (lambda pt, i: lambda e: e.tensor_copy(out=cqr[:, i, :], in_=pt[:, 0:128]))(pt, i), [pb], [B_cqr])
            for h in range(12):
                i, r = h // 4, 32 * (h % 4)
                dma(st_ring, Qs[h, 64, :].rearrange("(b t) -> b t", t=128), cqr[r:r + 32, i, :], [B_cqr], [dbuf("Qs")])
    S.barrier()

    if upto >= 3:
        with ExitStack() as ps3:
            phase_reset()
            kT_ring = Ring([sb(ps3, "kT%d" % i, [65, 8192], BF16) for i in range(2)])
            qT_ring = Ring([sb(ps3, "qT%d" % i, [65, 4096], BF16) for i in range(2)])
            vh_ring = Ring([sb(ps3, "vh%d" % i, [128, 64, 65], BF16) for i in range(2)])
            ob_ring = Ring([sb(ps3, "ob%d" % i, [64, T], BF16) for i in range(2)])
            hd_dma = S.ring("hd", 6)
            for t_, b_ in kT_ring.items:
                vec("dve", (lambda t_: lambda e: e.memset(t_[64:65, :], 1.0))(t_), [], [b_])
            s_banks = Ring(banks[0:5])
            o_banks = Ring(banks[5:7])
            bank_ring.items = banks[7:8]
            steps = []
            for h in range(12):
                for m in range(NT):
                    nblk = 4 * m + 4
                    for i in range(nblk):
                        for own in (True, False):
                            steps.append((h, m, i, own, i == 0 and own, (i == nblk - 1) and (not own)))
            LA = 3
            cur = {}
            sinfo = {}

            def load_head(hn):
                kt_, kb_ = kT_ring.next()
                qt_, qb_ = qT_ring.next()
                vt_, vb_ = vh_ring.next()
                dma(hd_dma, kt_[0:64, :], Ks[hn], [dbuf("Ks")], [kb_])
                dma(hd_dma, qt_[:], Qs[hn], [dbuf("Qs")], [qb_])
                dma(hd_dma, vt_[:], Vs[hn], [dbuf("Vs")], [vb_])
                cur[hn] = (kt_, kb_, qt_, qb_, vt_, vb_)

            def stage_a(idx):
                h, m, i, own, first, last = steps[idx]
                kt_, kb_, qt_, qb_, vt_, vb_ = cur[h]
                slot = i if own else 32 + i
                c0 = 0 if i < 4 * m else 128 * (i - 4 * m)
                diag = i >= 4 * m
                ps_, psb = s_banks.next()
                mm(ps_[:, c0:512], kt_[0:65, 128 * slot:128 * slot + 128], qt_[0:65, 512 * m + c0:512 * m + 512], True, not diag,
                   [kb_, qb_], [psb])
                if diag:
                    mk, mkb = (trineg_b, B_trineg_b) if own else (othmask_b, B_othmask_b)
                    mm(ps_[:, c0:c0 + 128], ident_b[:], mk[:], False, True, [B_ident_b, mkb], [psb])
                sinfo[idx] = (ps_, psb, slot, c0)

            def stage_b(idx):
                h, m, i, own, first, last = steps[idx]
                kt_, kb_, qt_, qb_, vt_, vb_ = cur[h]
                ps_, psb, slot, c0 = sinfo.pop(idx)
                if first:
                    cur["o"] = o_banks.next()
                po, pob = cur["o"]
                pt_, ptb = pT_ring.next()
                act(pt_[:, c0:512], ps_[:, c0:512], AF.Exp, [psb, B_biasK], [ptb], bias=biasK[:, slot, h:h + 1])
                mm(po[0:65, c0:512], vt_[:, slot, :], pt_[:, c0:512], first, last, [vb_, ptb], [pob])
                if last:
                    ot, otb = ob_ring.next()
                    normalize_out(po, pob, ot[:], otb)
                    dma(st_ring, CATs[h, :, 512 * m:512 * m + 512], ot[:], [otb], [dbuf("CATs")], eng="sp")

            load_head(0)
            load_head(1)
            late_casts([cur[1][1], cur[1][3], cur[1][5]])
            for idx in range(len(steps) + LA):
                if idx < len(steps):
                    stage_a(idx)
                if idx >= LA:
                    stage_b(idx - LA)
                    hh, mm_, _, _, _, last_ = steps[idx - LA]
                    if last_ and mm_ == NT - 1 and hh + 2 < 12:
                        load_head(hh + 2)
            bank_ring.items = banks
    S.barrier()

    if upto >= 4:
        with ExitStack() as ps5:
            phase_reset()
            alloc_work(ps5, nwk=3, share=True)
            wsT, B_wsT = sb(ps5, "wsT", [128, 12, 128], BF16)
            wsl, B_wsl = sb(ps5, "wsl", [128, 128], F32)
            bsp, B_bsp = sb(ps5, "bsp", [128, 6, 128], F32)
            vgc, B_vgc = sb(ps5, "vgc", [128, 6], F32)
            uT, B_uT = sb(ps5, "uT", [128, 6, T], F32)
            vg_ring = [sb(ps5, "vg%d" % i, [128, 768], F32) for i in range(2)]
            vsq, B_vsq = sb(ps5, "vsq", [128, 768], BF16)
            vss_ring = [sb(ps5, "vss%d" % i, [128, 12], F32) for i in range(4)]
            vn, B_vn = sb(ps5, "vn", [128, 4, 768], BF16)
            tokc, B_tokc = sb(ps5, "tokc", [128, 6, T], BF16)
            gtmp, B_gtmp = sb(ps5, "gtmp", [128, T], F32)
            mqn5, B_mqn5 = sb(ps5, "mqn5", [128, 2, T], BF16)
            moc, B_moc = sb(ps5, "moc", [64, 4, T], BF16)
            dma(ld_ring, bsp[:], bs_d[:, :, :], [], [B_bsp])
            dma(ld_ring, vgc[:], vgain_d[:, :], [], [B_vgc])
            for g in range(12):
                dma(ld_ring, wsl[:], ws_d[g], [], [B_wsl])
                pt, pb = bank_ring.next()
                vec("pe", (lambda pt: lambda e: e.transpose(out=pt[:, 0:128], in_=wsl[:], identity=ident_f[:]))(pt), [B_wsl, B_ident_f], [pb])
                vec("dve", (lambda pt, g: lambda e: e.tensor_tensor(out=wsT[:, g, :], in0=pt[:, 0:128], in1=tri_f[:], op=ALU.mult))(pt, g), [pb, B_tri_f], [B_wsT])

            def p5_pre_a(j):
                xT, B_xT = W["xT2"][j % 2]
                dma(ld_ring, xT[:], X[j], [dbuf("X%d" % j)], [B_xT])
                norm_a((xT, B_xT))

            def p5_post(j):
                xT, B_xT = W["xT2"][j % 2]
                hT, B_hT = W["hT2"]
                tw, bw = load_wk(wb_gmlp, "wgmlp_1536", 1536, 256)
                for pr in range(2):
                    pt, pb = bank_ring.next()
                    for k in range(KC):
                        mm(pt[:, :], tw[:, k, 128 * pr:128 * pr + 128], hT[:, k, :], k == 0, k == KC - 1, [bw, B_hT], [pb])
                    pair_norm(pt, pb, hcols8[:, 4:5], mqn5[:, pr, :], B_mqn5)
                tv1, bv1 = load_wk(wb_gmlp, "wgmlp_768", 768, 512)
                tv2, bv2 = load_wk(wb_gmlp, "wgmlp_1280", 1280, 256)

                def v_s1(bi):
                    pa, pab = bank_ring.next()
                    pb2, pbb = bank_ring.next()
                    for k in range(KC):
                        mm(pa[:, :], hT[:, k, 128 * bi:128 * bi + 128], tv1[:, k, 0:512], k == 0, k == KC - 1, [bv1, B_hT], [pab])
                    for k in range(KC):
                        mm(pb2[:, 0:256], hT[:, k, 128 * bi:128 * bi + 128], tv2[:, k, 0:256], k == 0, k == KC - 1, [bv2, B_hT], [pbb])
                    vg, B_vg = vg_ring[bi % 2]
                    vss, B_vss = vss_ring[bi]
                    act(vg[:, 0:512], pa[:, :], AF.Gelu_apprx_tanh, [pab], [B_vg])
                    act(vg[:, 512:768], pb2[:, 0:256], AF.Gelu_apprx_tanh, [pbb], [B_vg])
                    act(vsq[:], vg[:], AF.Square, [B_vg], [B_vsq])
                    vec("dve", lambda e: e.reduce_sum(out=vss[:], in_=vsq[:].rearrange("p (g d) -> p g d", g=12), axis=AX.X), [B_vsq], [B_vss])

                def v_s2(bi):
                    vg, B_vg = vg_ring[bi % 2]
                    vss, B_vss = vss_ring[bi]
                    act(vss[:], vss[:], AF.Ln, [B_vss], [B_vss], bias=EPS, scale=1.0 / 64)
                    act(vss[:], vss[:], AF.Exp, [B_vss], [B_vss], scale=-0.5)
                    vec("dve", lambda e: e.tensor_tensor(out=vn[:, bi, :].rearrange("p (g d) -> p g d", g=12),
                                                         in0=vg[:].rearrange("p (g d) -> p g d", g=12),
                                                         in1=vss[:].unsqueeze(2).to_broadcast([128, 12, 64]), op=ALU.mult),
                        [B_vg, B_vss], [B_vn])

                v_s1(0)
                v_s1(1)
                v_s2(0)
                v_s1(2)
                v_s2(1)
                v_s1(3)
                v_s2(2)
                v_s2(3)
                for (c0, npair, p0) in ((0, 4, 0), (512, 2, 4)):
                    tw, bw = load_wk(wb_gmlp, "wgmlp_%d" % c0, c0, 128 * npair)
                    for pi in range(npair):
                        pr = p0 + pi
                        pt, pb = bank_ring.next()
                        for k in range(KC):
                            mm(pt[:, :], tw[:, k, 128 * pi:128 * pi + 128], hT[:, k, :], k == 0, k == KC - 1, [bw, B_hT], [pb])
                        act(uT[:, pr, :], pt[:, :], AF.Gelu_apprx_tanh, [pb], [B_uT])
                mem_attn(1, mqn5, B_mqn5, lambda hh: (moc[:, hh, :], B_moc))
                for pr in range(6):
                    pt, pb = bank_ring.next()
                    for bi in range(4):
                        for gg in range(2):
                            g = 2 * pr + gg
                            mm(pt[64 * gg:64 * gg + 64, 128 * bi:128 * bi + 128], vn[:, bi, 64 * g:64 * g + 64], wsT[:, g, :], True, True, [B_vn, B_wsT], [pb])
                    vec("dve", (lambda pt, pr: lambda e: e.scalar_tensor_tensor(out=gtmp[:].rearrange("p (b t) -> p b t", b=4),
                                                                               in0=pt[:, :].rearrange("p (b t) -> p b t", b=4),
                                                                               scalar=vgc[:, pr:pr + 1],
                                                                               in1=bsp[:, pr, :].unsqueeze(1).to_broadcast([128, 4, 128]),
                                                                               op0=ALU.mult, op1=ALU.add))(pt, pr),
                        [pb, B_bsp, B_vgc], [B_gtmp])
                    vec("dve", (lambda pr: lambda e: e.tensor_tensor(out=tokc[:, pr, :], in0=gtmp[:], in1=uT[:, pr, :], op=ALU.mult))(pr),
                        [B_gtmp, B_uT], [B_tokc])
                for dp in range(4):
                    t0_, b = W["wk_ring"].next()
                    t = t0_[:, :, :].rearrange("p k c -> p (k c)")[:, 0:2560].rearrange("p (c n) -> p c n", n=256)
                    src = wb_o[1][0:768, 256 * dp:256 * dp + 256].rearrange("(c p) n -> p c n", p=128)
                    dma(wk_dma, t[:, 0:6, :], src, [dbuf("wo1_%d" % dp)], [b])
                    src2 = wb_o[1][768:1024, 256 * dp:256 * dp + 256].rearrange("(c p) n -> p c n", p=64)
                    dma(wk_dma, t[0:64, 6:10, :], src2, [dbuf("wo1_%d" % dp)], [b])
                    for dd in range(2):
                        do = 2 * dp + dd
                        py, pyb = bank_ring.next()
                        for c in range(6):
                            mm(py[:, :], t[:, c, 128 * dd:128 * dd + 128], tokc[:, c, :], c == 0, False, [b, B_tokc], [pyb])
                        for c in range(4):
                            mm(py[:, :], t[0:64, 6 + c, 128 * dd:128 * dd + 128], moc[0:64, c, :], False, c == 3, [b, B_moc], [pyb])
                        vec("dve", (lambda do, py: lambda e: e.tensor_tensor(out=xT[:, do, :], in0=py[:, :], in1=xT[:, do, :], op=ALU.add))(do, py),
                            [pyb, B_xT], [B_xT])
                dma(stb_ring, X[j], xT[:], [B_xT], [dbuf("X%d" % j)])

            cat = uT[:, :, :].rearrange("p a b -> p (a b)")[:, 0:2048].bitcast(BF16).rearrange("p (c t) -> p c t", c=8)
            B_cat = B_uT
            def p4_pre_a(j):
                xT, B_xT = W["xT2"][j % 2]
                dma(ld_ring, xT[:], X[j], [dbuf("X%d" % j)], [B_xT])
                dma(ld_ring, cat[:], CATs[:, :, 512 * j:512 * j + 512].rearrange("(c two) p t -> (two p) c t", two=2), [dbuf("CATs")], [B_cat])
                for dp in range(4):
                    t0_, b = W["wk_ring"].next()
                    t = t0_[:, :, :].rearrange("p k c -> p (k c)")[:, 0:2048].rearrange("p (c n) -> p c n", n=256)
                    src = wb_o[0][:, 256 * dp:256 * dp + 256].rearrange("(c p) n -> p c n", p=128)
                    dma(wk_dma, t[:, 0:8, :], src, [dbuf("wo0_%d" % dp)], [b])
                    for dd in range(2):
                        do = 2 * dp + dd
                        py, pyb = bank_ring.next()
                        for c in range(8):
                            mm(py[:, :], t[:, c, 128 * dd:128 * dd + 128], cat[:, c, :], c == 0, c == 7, [b, B_cat], [pyb])
                        vec("dve", (lambda do, py: lambda e: e.tensor_tensor(out=xT[:, do, :], in0=py[:, :], in1=xT[:, do, :], op=ALU.add))(do, py),
                            [pyb, B_xT], [B_xT])
                norm_a((xT, B_xT))

            def p4_post(j):
                xT, B_xT = W["xT2"][j % 2]
                dma(stb_ring, X[j], xT[:], [B_xT], [dbuf("X%d" % j)])

            def p6_pre_a(j):
                xT, B_xT = W["xT2"][j % 2]
                dma(ld_ring, xT[:], X[j], [dbuf("X%d" % j)], [B_xT])
                norm_a((xT, B_xT))

            def p6_post(j):
                xT, B_xT = W["xT2"][j % 2]
                dma(stb_ring, out_d[j], xT[:], [B_xT], [dbuf("out")])


            PRE_A = (p4_pre_a, p5_pre_a, p6_pre_a)
            POST = (p4_post, p5_post, p6_post)
            GCOL = (16, 24, 40)
            KEYS = ((2, 0), (1, 1), (2, 1))

            run_tiles(3 * NT,
                      lambda v: PRE_A[v // NT](v % NT),
                      lambda v: norm_b(W["xT2"][v % 2], GCOL[v // NT], W["hT"]),
                      lambda v, hook: ffn_ab(KEYS[v // NT], hook),
                      lambda v, hook, yp: ffn_y(KEYS[v // NT], W["xT2"][v % 2], hook, yp),
                      lambda v: norm_a(W["xT2"][v % 2]) if v // NT == 1 else None,
                      lambda v: norm_b(W["xT2"][v % 2], 32, W["hT2"]) if v // NT == 1 else None,
                      lambda v: POST[v // NT](v % NT),
                      lambda v: KEYS[v // NT])
    S.barrier()

    S.finalize()
    with nc.Block() as block:
        @block.tensor
        def _(eng):
            S.emit_engine("pe", eng)

        @block.scalar
        def _(eng):
            S.emit_engine("act", eng)

        @block.vector
        def _(eng):
            S.emit_engine("dve", eng)

        @block.gpsimd
        def _(eng):
            S.emit_engine("pool", eng)

        @block.sync
        def _(eng):
            S.emit_engine("sp", eng)
    es.close()
    return nc


def _host_inputs(inp, c):
    b, p = c // 2, c % 2
    f32 = np.float32
    x = np.asarray(inp["x"], f32)[b].reshape(64, 128, D)
    m = {}
    m["x_own"] = np.ascontiguousarray(x[p::2].reshape(NT, T, KC, 128).transpose(0, 3, 2, 1))
    m["x_oth"] = np.ascontiguousarray(x[1 - p::2].reshape(NT, T, KC, 128).transpose(0, 3, 2, 1))
    m["mem"] = np.ascontiguousarray(np.asarray(inp["mem"], f32)[b])
    ffn_w = {1: (inp["ffn1_w_in"], inp["ffn1_w_out"]), 2: (inp["ffn2_w_in"], inp["ffn2_w_out"])}
    for i in (1, 2):
        for L in (0, 1):
            m["ffn%d_w_in_%d" % (i, L)] = np.ascontiguousarray(np.asarray(ffn_w[i][0], f32)[L])
            m["ffn%d_w_out_%d" % (i, L)] = np.ascontiguousarray(np.asarray(ffn_w[i][1], f32)[L])
    for L in (0, 1):
        m["w_out_%d" % L] = np.ascontiguousarray(np.asarray(inp["w_out"], f32)[L])
        m["mem_w_kv_%d" % L] = np.ascontiguousarray(np.asarray(inp["mem_w_kv"], f32)[L])
    m["fox_w_in"] = np.ascontiguousarray(np.asarray(inp["fox_w_in"], f32)[0])
    m["gmlp_w_in"] = np.ascontiguousarray(np.asarray(inp["gmlp_w_in"], f32)[0])
    g = []
    for L in (0, 1):
        for nm in ("norm_ffn1", "norm_mix", "norm_ffn2"):
            g.append(np.asarray(inp[nm], f32)[L].reshape(8, 128).T)
    m["gains"] = np.ascontiguousarray(np.concatenate(g, axis=1))
    fq = np.asarray(inp["fox_q_norm"], f32)[0]
    fk = np.asarray(inp["fox_k_norm"], f32)[0]
    mq = np.asarray(inp["mem_q_norm"], f32)
    mk = np.asarray(inp["mem_k_norm"], f32)
    cols = [fq, fk, mq[0], mk[0], mq[1], mk[1], fq, fk]
    m["hcols"] = np.ascontiguousarray(np.stack([np.tile(v, 2) for v in cols], axis=1))
    m["hrows"] = np.ascontiguousarray(np.broadcast_to(np.concatenate([fq, fk])[None, :], (128, 128)))
    m["hrows2"] = np.ascontiguousarray(np.broadcast_to(np.concatenate([mq[0], mk[0], mq[1], mk[1]])[None, :], (128, 256)))
    m["bf_b"] = np.ascontiguousarray(np.broadcast_to(np.asarray(inp["fox_b_f"], f32)[0][None, :], (128, 12)))
    m["memnorm_b"] = np.ascontiguousarray(np.broadcast_to(np.asarray(inp["mem_norm"], f32)[None, :], (128, D)))
    m["vgain_c"] = np.ascontiguousarray(np.asarray(inp["gmlp_v_norm"], f32)[0].reshape(6, 128).T)
    m["gmlp_w_s"] = np.ascontiguousarray(np.asarray(inp["gmlp_w_s"], f32)[0])
    bs = np.asarray(inp["gmlp_b_s"], f32)[0]
    m["bs_pair"] = np.ascontiguousarray(np.repeat(bs.reshape(6, 2, 1, 128), 64, axis=2).reshape(6, 128, 128).transpose(1, 0, 2))
    m["ident"] = np.eye(128, dtype=f32)
    s_idx = np.arange(128)[:, None]
    t_idx = np.arange(128)[None, :]
    m["tri01"] = (s_idx <= t_idx).astype(f32)
    m["trineg"] = np.where(s_idx <= t_idx, 0.0, NEG).astype(f32)
    m["othmask"] = np.full((128, 128), NEG if p == 0 else 0.0, f32)
    bd = np.zeros((128, 128), f32)
    bd[0:64, 0:64] = 1.0
    bd[64:128, 64:128] = 1.0
    m["bdiag"] = bd
    m["pflag"] = np.ascontiguousarray(np.broadcast_to(np.array([p, 1 - p], f32)[None, :], (128, 2)))
    return m


_NC_CACHE = {}


def kernel(**inputs):
    if "nc" not in _NC_CACHE:
        _NC_CACHE["nc"] = build()
    nc = _NC_CACHE["nc"]
    in_maps = [_host_inputs(inputs, c) for c in range(8)]
    res = run_bass_kernel_spmd(nc, in_maps, core_ids=list(range(8)))
    out = np.zeros((4, 64, 128, D), np.float32)
    for c in range(8):
        b, p = c // 2, c % 2
        o = np.asarray(res.results[c]["out"], np.float32)
        out[b, p::2] = o.transpose(0, 3, 2, 1).reshape(32, 128, D)
    return out.reshape(4, 8192, D)
```

```python
import numpy as np
import ml_dtypes
from contextlib import ExitStack
import concourse.bass as bass
import concourse.mybir as mybir
from concourse.bass_utils import run_bass_kernel_spmd

F32 = mybir.dt.float32
BF16 = mybir.dt.bfloat16
AF = mybir.ActivationFunctionType
ALU = mybir.AluOpType
AX = mybir.AxisListType

D = 1024
KC = 8
FF = 2816
FC = 22
T = 512
NT = 8
NBLK = 32
EPS = 1e-6
NEG = -30000.0

ENGS = ("pe", "act", "dve", "pool", "sp")


class Buf:
    __slots__ = ("name", "w", "r")

    def __init__(self, name):
        self.name = name
        self.w = {}
        self.r = {}


class Op:
    __slots__ = ("eng", "emit", "deps", "signal", "sigval", "stream", "dval", "key")


class Stream:
    def __init__(self, sem):
        self.sem = sem
        self.count = 0
        self.last = None


class Sched:
    def __init__(self, nc, es):
        self.nc = nc
        self.es = es
        self.ops = {e: [] for e in ENGS}
        self.sem = {e: es.enter_context(nc.semaphore("sem_" + e)) for e in ENGS}
        self.streams = []
        self.nsem = 0

    def ring(self, name, n):
        out = []
        for i in range(n):
            s = Stream(self.es.enter_context(self.nc.semaphore("dq_%s_%d" % (name, i))))
            self.streams.append(s)
            out.append(s)
        return Ring(out)

    def add(self, eng, emit, reads=(), writes=(), stream=None):
        op = Op()
        op.eng = eng
        op.emit = emit
        op.deps = []
        op.signal = False
        op.sigval = 0
        op.stream = stream
        op.dval = 0
        op.key = eng if stream is None else ("dma", id(stream))
        for b in reads:
            op.deps.extend(b.w.values())
        for b in writes:
            op.deps.extend(b.w.values())
            op.deps.extend(b.r.values())
        if stream is not None:
            if stream.last is not None:
                op.deps.append(stream.last)
            stream.count += 1
            op.dval = 16 * stream.count
            stream.last = op
        for b in reads:
            b.r[op.key] = op
        for b in writes:
            b.w = {op.key: op}
            b.r = {}
        self.ops[eng].append(op)
        return op

    def barrier(self):
        lasts = [self.ops[e][-1] for e in ENGS if self.ops[e]]
        lasts = [o for o in lasts if o.emit is not None]
        real = []
        for e in ENGS:
            for o in reversed(self.ops[e]):
                if o.emit is not None and o.stream is None:
                    real.append(o)
                    break
        dl = [s.last for s in self.streams if s.last is not None]
        for e in ENGS:
            op = Op()
            op.eng = e
            op.emit = None
            op.deps = list(real) + list(dl)
            op.signal = False
            op.sigval = 0
            op.stream = None
            op.dval = 0
            op.key = e
            self.ops[e].append(op)

    def finalize(self):
        for e in ENGS:
            for op in self.ops[e]:
                for d in op.deps:
                    if d.stream is None:
                        if d.eng == "pe" and op.eng == "pe" and op.stream is None:
                            continue
                        d.signal = True
        for e in ENGS:
            c = 0
            for op in self.ops[e]:
                if op.signal:
                    c += 1
                    op.sigval = c
            assert c < 60000, (e, c)
        for s in self.streams:
            assert 16 * s.count < 60000, s.count

    def emit_engine(self, e, eng):
        known = {}
        for op in self.ops[e]:
            need = {}
            for d in op.deps:
                if d.stream is None:
                    if d.eng == "pe" and e == "pe" and op.stream is None:
                        continue
                    sem, val = self.sem[d.eng], d.sigval
                else:
                    sem, val = d.stream.sem, d.dval
                k = id(sem)
                if val > need.get(k, (None, 0))[1]:
                    need[k] = (sem, val)
            for k, (sem, val) in need.items():
                if known.get(k, 0) >= val:
                    continue
                eng.wait_ge(sem, val)
                known[k] = val
            if op.emit is None:
                continue
            ins = op.emit(eng)
            if op.stream is not None:
                ins.then_inc(op.stream.sem, 16)
            elif op.signal:
                ins.then_inc(self.sem[e], 1)


class Ring:
    def __init__(self, items):
        self.items = items
        self.i = 0

    def next(self):
        x = self.items[self.i % len(self.items)]
        self.i += 1
        return x


def build(debug=False, upto=99):
    nc = bass.Bass("TRN2", target_bir_lowering=False)
    es = ExitStack()
    S = Sched(nc, es)

    def din(name, shape, dt=F32):
        return nc.dram_tensor(name, list(shape), dt, kind="ExternalInput").ap()

    def dscr(name, shape, dt, dump=False):
        kind = "ExternalOutput" if (debug and dump) else "Internal"
        return nc.dram_tensor(name, list(shape), dt, kind=kind).ap()

    x_own = din("x_own", [NT, 128, KC, T])
    x_oth = din("x_oth", [NT, 128, KC, T])
    mem_in = din("mem", [256, D])
    w_ffn_in = {(i, L): din("ffn%d_w_in_%d" % (i, L), [D, 2 * FF]) for i in (1, 2) for L in (0, 1)}
    w_ffn_out = {(i, L): din("ffn%d_w_out_%d" % (i, L), [FF, D]) for i in (1, 2) for L in (0, 1)}
    w_o = {L: din("w_out_%d" % L, [D, D]) for L in (0, 1)}
    w_memkv = {L: din("mem_w_kv_%d" % L, [D, 512]) for L in (0, 1)}
    w_fox = din("fox_w_in", [D, 2572])
    w_gmlp = din("gmlp_w_in", [D, 1792])
    gains_d = din("gains", [128, 48])
    hcols_d = din("hcols", [128, 8])
    hrows_d = din("hrows", [128, 128])
    bf_d = din("bf_b", [128, 12])
    memnorm_d = din("memnorm_b", [128, D])
    vgain_d = din("vgain_c", [128, 6])
    ws_d = din("gmlp_w_s", [12, 128, 128])
    bs_d = din("bs_pair", [128, 6, 128])
    ident_d = din("ident", [128, 128])
    tri_d = din("tri01", [128, 128])
    trineg_d = din("trineg", [128, 128])
    othmask_d = din("othmask", [128, 128])
    bdiag_d = din("bdiag", [128, 128])
    pflag_d = din("pflag", [128, 2])
    out_d = nc.dram_tensor("out", [NT, 128, KC, T], F32, kind="ExternalOutput").ap()

    wb_ffn_in = {k: dscr("b_ffn%d_in_%d" % k, [D, 2 * FF], BF16) for k in w_ffn_in}
    wb_ffn_out = {k: dscr("b_ffn%d_out_%d" % k, [FF, D], BF16) for k in w_ffn_out}
    wb_o = {L: dscr("b_wo_%d" % L, [D, D], BF16) for L in (0, 1)}
    wb_memkv = {L: dscr("b_memkv_%d" % L, [D, 512], BF16) for L in (0, 1)}
    wb_fox = dscr("b_fox", [D, 2572], BF16)
    wb_gmlp = dscr("b_gmlp", [D, 1792], BF16)
    X = dscr("X", [NT, 128, KC, T], F32, dump=True)
    Ks = dscr("Ks", [12, 64, 8192], BF16, dump=True)
    Vs = dscr("Vs", [12, 128, 64, 65], BF16, dump=True)
    Qs = dscr("Qs", [12, 65, 4096], BF16, dump=True)
    CATs = dscr("CATs", [16, 64, 4096], BF16, dump=True)
    dbg_bias = dscr("dbg_bias", [128, 768], F32, dump=True) if debug else None

    dram_bufs = {}

    def dbuf(name):
        if name not in dram_bufs:
            dram_bufs[name] = Buf(name)
        return dram_bufs[name]

    ARENA = 211968
    PBASE = 24 * 1024
    arena = es.enter_context(nc.sbuf_tensor("arena", [128, ARENA], mybir.dt.uint8))
    tops = {"p": 0, "f": PBASE}
    isz = {F32: 4, BF16: 2}

    def phase_reset():
        tops["f"] = PBASE

    def sb(stack, name, shape, dt):
        n = isz[dt]
        for d_ in shape[1:]:
            n *= d_
        n_exact = n
        n = (n + 31) // 32 * 32
        which = "p" if stack is es else "f"
        off = tops[which]
        tops[which] = off + n
        n = n_exact
        assert tops["p"] <= PBASE, (name, tops)
        assert tops["f"] <= ARENA, (name, tops)
        ap = arena[0:shape[0], off:off + n].bitcast(dt)
        if len(shape) == 3:
            ap = ap.rearrange("p (a b) -> p a b", a=shape[1])
        elif len(shape) == 4:
            ap = ap.rearrange("p (a b c) -> p a b c", a=shape[1], b=shape[2])
        return ap, Buf(name)

    class _Rec:
        def __getattr__(self, name):
            def f(*a, **kw):
                self.call = (name, a, kw)
            return f

    banks = []
    for i in range(8):
        t = es.enter_context(nc.psum_tensor("bank%d" % i, [128, 512], F32))
        banks.append((t, Buf("bank%d" % i)))
    bank_ring = Ring(banks)

    def dma(ring, out, in_, reads, writes, eng=None):
        if eng is None:
            eng = "pool" if (ring is st_ring or ring is stb_ring) else "sp"
        st = ring.next()
        return S.add(eng, lambda e: e.dma_start(out=out, in_=in_), reads=reads, writes=writes, stream=st)

    def mm(out, lhsT, rhs, start, stop, reads, writes):
        return S.add("pe", lambda e: e.matmul(out, lhsT, rhs, start=start, stop=stop), reads=reads, writes=writes)

    def act(out, in_, func, reads, writes, bias=None, scale=None):
        kw = {}
        if bias is not None:
            kw["bias"] = bias
        if scale is not None:
            kw["scale"] = scale
        return S.add("act", lambda e: e.activation(out=out, in_=in_, func=func, **kw), reads=reads, writes=writes)

    def vec(eng, fn, reads, writes):
        r = _Rec()
        fn(r)
        name, a, kw = r.call
        return S.add(eng, lambda e: getattr(e, name)(*a, **kw), reads=reads, writes=writes)

    ld_ring = S.ring("ld", 8)
    cast_ring = S.ring("cast", 8)
    st_ring = S.ring("st", 8)
    stb_ring = S.ring("stb", 4)

    ident_f, B_ident_f = sb(es, "ident_f", [128, 128], F32)
    ident_b, B_ident_b = sb(es, "ident_b", [128, 128], BF16)
    ones_b, B_ones_b = sb(es, "ones_b", [128, 128], BF16)
    ones_f, B_ones_f = sb(es, "ones_f", [128, 128], F32)
    bdiag_b, B_bdiag_b = sb(es, "bdiag_b", [128, 128], BF16)
    tri_f, B_tri_f = sb(es, "tri_f", [128, 128], F32)
    trineg_b, B_trineg_b = sb(es, "trineg_b", [128, 128], BF16)
    othmask_b, B_othmask_b = sb(es, "othmask_b", [128, 128], BF16)
    gains, B_gains = sb(es, "gains", [128, 48], F32)
    hcols, B_hcols = sb(es, "hcols", [128, 8], F32)
    hcols8, B_hcols8 = sb(es, "hcols8", [128, 8], F32)
    pflag, B_pflag = sb(es, "pflag", [128, 2], F32)
    tmpc, B_tmpc = sb(es, "tmpc", [128, 128], F32)
    kmT = {}
    VmP = {}
    for L in (0, 1):
        kmT[L] = sb(es, "kmT%d" % L, [128, 2, 256], BF16)
        VmP[L] = sb(es, "VmP%d" % L, [128, 2, 4, 65], BF16)
    Mcol, B_Mcol = sb(es, "Mcol", [128, 4], F32)

    dma(ld_ring, ident_f[:], ident_d[:, :], [], [B_ident_f])
    dma(ld_ring, tri_f[:], tri_d[:, :], [], [B_tri_f])
    dma(ld_ring, gains[:], gains_d[:, :], [], [B_gains])
    dma(ld_ring, hcols[:], hcols_d[:, :], [], [B_hcols])
    dma(ld_ring, pflag[:], pflag_d[:, :], [], [B_pflag])
    vec("dve", lambda e: e.tensor_copy(out=ident_b[:], in_=ident_f[:]), [B_ident_f], [B_ident_b])
    vec("dve", lambda e: e.memset(ones_b[:], 1.0), [], [B_ones_b])
    vec("dve", lambda e: e.memset(ones_f[:], 1.0), [], [B_ones_f])
    dma(ld_ring, tmpc[:], bdiag_d[:, :], [], [B_tmpc])
    vec("dve", lambda e: e.tensor_copy(out=bdiag_b[:], in_=tmpc[:]), [B_tmpc], [B_bdiag_b])
    dma(ld_ring, tmpc[:], trineg_d[:, :], [], [B_tmpc])
    vec("dve", lambda e: e.tensor_copy(out=trineg_b[:], in_=tmpc[:]), [B_tmpc], [B_trineg_b])
    dma(ld_ring, tmpc[:], othmask_d[:, :], [], [B_tmpc])
    vec("dve", lambda e: e.tensor_copy(out=othmask_b[:], in_=tmpc[:]), [B_tmpc], [B_othmask_b])
    vec("dve", lambda e: e.tensor_scalar_mul(out=hcols8[:], in0=hcols[:], scalar1=0.125), [B_hcols], [B_hcols8])
    dma(ld_ring, tmpc[:], hrows_d[:, :], [], [B_tmpc])
    mx, B_mx = sb(es, "mx", [128, 8], F32)
    for i in range(2):
        vec("dve", (lambda i: lambda e: e.reduce_max(out=mx[:, i:i + 1], in_=tmpc[:, 64 * i:64 * i + 64],
                                                     axis=AX.X, apply_absolute_value=True))(i), [B_tmpc], [B_mx])
    hrows2_d = din("hrows2", [128, 256])
    tmpc2, B_tmpc2 = sb(es, "tmpc2", [128, 256], F32)
    dma(ld_ring, tmpc2[:], hrows2_d[:, :], [], [B_tmpc2])
    for i in range(4):
        vec("dve", (lambda i: lambda e: e.reduce_max(out=mx[:, 2 + i:3 + i], in_=tmpc2[:, 64 * i:64 * i + 64],
                                                     axis=AX.X, apply_absolute_value=True))(i), [B_tmpc2], [B_mx])
    for j in range(3):
        vec("dve", (lambda j: lambda e: e.tensor_tensor(out=Mcol[:, j:j + 1], in0=mx[:, 2 * j:2 * j + 1], in1=mx[:, 2 * j + 1:2 * j + 2], op=ALU.mult))(j),
            [B_mx], [B_Mcol])
    vec("dve", lambda e: e.tensor_scalar_mul(out=Mcol[:, 0:3], in0=Mcol[:, 0:3], scalar1=-8.0), [B_Mcol], [B_Mcol])

    cast_gate = [[]]

    def cast_cols(dst, src, bname, c0, ncols):
        dma(cast_ring, dst[:, c0:c0 + ncols], src[:, c0:c0 + ncols], list(cast_gate[0]), [dbuf(bname)], eng="pool")

    FOX_G = ((768, 512), (1280, 256), (0, 512), (512, 256), (1536, 512), (2048, 268), (2316, 256))
    GMLP_G = ((0, 512), (512, 256), (768, 512), (1280, 256), (1536, 256))

    def cast_ffn(key):
        for g in range(6):
            nc_ = 512 if g < 5 else 256
            cast_cols(wb_ffn_in[key], w_ffn_in[key], "wfi%d%d_a%d" % (key + (g,)), 512 * g, nc_)
            cast_cols(wb_ffn_in[key], w_ffn_in[key], "wfi%d%d_b%d" % (key + (g,)), FF + 512 * g, nc_)
        for dp in range(4):
            cast_cols(wb_ffn_out[key], w_ffn_out[key], "wfo%d%d_%d" % (key + (dp,)), 256 * dp, 256)

    cast_cols(wb_memkv[0], w_memkv[0], "wmemkv0", 0, 512)
    cast_cols(wb_memkv[1], w_memkv[1], "wmemkv1", 0, 512)
    cast_ffn((1, 0))
    for c0, ncl in FOX_G:
        cast_cols(wb_fox, w_fox, "wfox_%d" % c0, c0, ncl)

    def late_casts(gate):
        cast_gate[0] = gate
        for dp in range(4):
            cast_cols(wb_o[0], w_o[0], "wo0_%d" % dp, 256 * dp, 256)
        cast_ffn((2, 0))
        cast_ffn((1, 1))
        for c0, ncl in GMLP_G:
            cast_cols(wb_gmlp, w_gmlp, "wgmlp_%d" % c0, c0, ncl)
        for dp in range(4):
            cast_cols(wb_o[1], w_o[1], "wo1_%d" % dp, 256 * dp, 256)
        cast_ffn((2, 1))
        cast_gate[0] = []

    W = {}

    def alloc_work(stack, nwk=4, share=False, nsilu=2):
        W["xT2"] = [sb(stack, "xT%d" % i, [128, KC, T], F32) for i in range(2)]
        W["hT"] = sb(stack, "hT", [128, KC, T], BF16)
        W["hT2"] = sb(stack, "hT2", [128, KC, T], BF16)
        W["sq"] = sb(stack, "sq", [128, KC, T], BF16)
        W["lnt"] = sb(stack, "lnt", [128, T], F32)
        W["rstd"] = sb(stack, "rstd", [128, T], F32)
        W["sq2"] = sb(stack, "sq2", [128, T], BF16)
        W["lnt2"] = W["lnt"] if share else sb(stack, "lnt2", [128, T], F32)
        W["rstd2"] = W["rstd"] if share else sb(stack, "rstd2", [128, T], F32)
        W["gT"] = sb(stack, "gT", [128, FC, T], BF16)
        W["s_ring"] = Ring([sb(stack, "silu%d" % i, [128, T], F32) for i in range(nsilu)])
        W["wk_ring"] = Ring([sb(stack, "wk%d" % i, [128, KC, 512], BF16) for i in range(nwk)])
        W["wo_ring"] = Ring([sb(stack, "wo%d" % i, [128, FC, 256], BF16) for i in range(2)])

    wk_dma = S.ring("wk", 4)
    wo_dma = S.ring("wo", 2)

    def load_wk(wsrc, bname, c0, ncols):
        t, b = W["wk_ring"].next()
        src = wsrc[:, c0:c0 + ncols].rearrange("(k p) c -> p k c", p=128)
        dma(wk_dma, t[:, :, 0:ncols], src, [dbuf(bname)], [b])
        return t, b

    def norm_a(xTt):
        xT, B_xT = xTt
        sq, B_sq = W["sq"]
        act(sq[:], xT[:], AF.Square, [B_xT], [B_sq])

    def norm_b(xTt, gcol0, hTt):
        xT, B_xT = xTt
        hT, B_hT = hTt
        sq, B_sq = W["sq"]
        lnt, B_lnt = W["lnt"]
        rstd, B_rstd = W["rstd"]
        pt, pb = bank_ring.next()
        for k in range(KC):
            mm(pt[:, :], ones_b[:], sq[:, k, :], k == 0, k == KC - 1, [B_ones_b, B_sq], [pb])
        act(lnt[:], pt[:, :], AF.Ln, [pb], [B_lnt], bias=EPS, scale=1.0 / D)
        act(rstd[:], lnt[:], AF.Exp, [B_lnt], [B_rstd], scale=-0.5)
        for k in range(KC):
            vec("dve", (lambda k: lambda e: e.scalar_tensor_tensor(out=hT[:, k, :], in0=xT[:, k, :], scalar=gains[:, gcol0 + k:gcol0 + k + 1],
                                                                in1=rstd[:], op0=ALU.mult, op1=ALU.mult))(k),
                [B_xT, B_gains, B_rstd], [B_hT])

    def ffn_ab(key, hook=None):
        hT, B_hT = W["hT"]
        gT, B_gT = W["gT"]
        win = wb_ffn_in[key]
        for g in range(6):
            nf = 4 if g < 5 else 2
            ta, ba = load_wk(win, "wfi%d%d_a%d" % (key + (g,)), 512 * g, 128 * nf)
            tb, bb = load_wk(win, "wfi%d%d_b%d" % (key + (g,)), FF + 512 * g, 128 * nf)
            for fi in range(nf):
                f = 4 * g + fi
                pa, pba = bank_ring.next()
                pbt, pbb = bank_ring.next()
                for k in range(KC):
                    mm(pa[:, :], ta[:, k, 128 * fi:128 * fi + 128], hT[:, k, :], k == 0, k == KC - 1, [ba, B_hT], [pba])
                for k in range(KC):
                    mm(pbt[:, :], tb[:, k, 128 * fi:128 * fi + 128], hT[:, k, :], k == 0, k == KC - 1, [bb, B_hT], [pbb])
                st, sbf = W["s_ring"].next()
                act(st[:], pa[:, :], AF.Silu, [pba], [sbf])
                vec("dve", (lambda f, st, pbt: lambda e: e.tensor_tensor(out=gT[:, f, :], in0=pbt[:, :], in1=st[:], op=ALU.mult))(f, st, pbt),
                    [pbb, sbf], [B_gT])
            if g == 1 and hook is not None:
                hook()

    def ffn_y_load(key, dp):
        t, b = W["wo_ring"].next()
        src = wb_ffn_out[key][:, 256 * dp:256 * dp + 256].rearrange("(f p) c -> p f c", p=128)
        dma(wo_dma, t[:, :, :], src, [dbuf("wfo%d%d_%d" % (key + (dp,)))], [b])
        return t, b

    def ffn_y_prefetch(key):
        return [ffn_y_load(key, 0), ffn_y_load(key, 1)]

    def ffn_y(key, xTt, hook=None, pre=None):
        xT, B_xT = xTt
        gT, B_gT = W["gT"]
        pre = list(pre) if pre else []
        for dp in range(4):
            t, b = pre.pop(0) if pre else ffn_y_load(key, dp)
            for dd in range(2):
                do = 2 * dp + dd
                py, pyb = bank_ring.next()
                for f in range(FC):
                    mm(py[:, :], t[:, f, 128 * dd:128 * dd + 128], gT[:, f, :], f == 0, f == FC - 1, [b, B_gT], [pyb])
                vec("dve", (lambda do, py: lambda e: e.scalar_tensor_tensor(out=xT[:, do, :], in0=py[:, :], scalar=0.5, in1=xT[:, do, :],
                                                                           op0=ALU.mult, op1=ALU.add))(do, py),
                    [pyb, B_xT], [B_xT])
            if dp == 1 and hook is not None:
                hook()

    def pair_norm(pt, pb, gcol, out_ap, out_buf, n=T):
        sq2, B_sq2 = W["sq2"]
        lnt2, B_lnt2 = W["lnt2"]
        rstd2, B_rstd2 = W["rstd2"]
        act(sq2[:, 0:n], pt[:, 0:n], AF.Square, [pb], [B_sq2])
        p2, p2b = bank_ring.next()
        mm(p2[:, 0:n], bdiag_b[:], sq2[:, 0:n], True, True, [B_bdiag_b, B_sq2], [p2b])
        act(lnt2[:, 0:n], p2[:, 0:n], AF.Ln, [p2b], [B_lnt2], bias=EPS, scale=1.0 / 64)
        act(rstd2[:, 0:n], lnt2[:, 0:n], AF.Exp, [B_lnt2], [B_rstd2], scale=-0.5)
        vec("dve", lambda e: e.scalar_tensor_tensor(out=out_ap, in0=pt[:, 0:n], scalar=gcol, in1=rstd2[:, 0:n], op0=ALU.mult, op1=ALU.mult),
            [pb, B_rstd2, B_hcols, B_hcols8], [out_buf])

    def run_tiles(n, pre_a, pre_b, ab, y, mid_a, mid_b, post, key):
        pre_a(0)
        pre_b(0)
        ab(0, None)
        yp = ffn_y_prefetch(key)
        for j in range(n):
            if j + 1 < n:
                pre_a(j + 1)
            y(j, (lambda j=j: pre_b(j + 1)) if j + 1 < n else None, yp)
            mid_a(j)
            if j + 1 < n:
                ab(j + 1, (lambda j=j: mid_b(j)))
                yp = ffn_y_prefetch(key)
            else:
                mid_b(j)
            post(j)

    def nop(*a):
        return None

    pT_ring = Ring([sb(es, "pT%d" % i, [128, T], BF16) for i in range(4)])
    o_sb, B_o_sb = sb(es, "o_sb", [65, T], F32)
    rec, B_rec = sb(es, "rec", [65, T], F32)

    def normalize_out(po, pob, out_ap, out_buf, bank=None):
        vec("dve", lambda e: e.reciprocal(out=rec[64:65, :], in_=po[64:65, :]), [pob], [B_rec])
        act(o_sb[0:64, :], po[0:64, :], AF.Copy, [pob, B_rec], [B_o_sb])
        pc, pcb = bank_ring.next() if bank is None else bank
        mm(pc[0:64, :], ones_f[64:65, 0:64], rec[64:65, :], True, True, [B_ones_f, B_rec], [pcb])
        vec("dve", lambda e: e.tensor_tensor(out=out_ap, in0=pc[0:64, :], in1=o_sb[0:64, :], op=ALU.mult), [pcb, B_o_sb], [out_buf])

    def mem_attn(L, mqn, B_mqn, out_fn, after=None):
        kt, kb = kmT[L]
        vt, vb = VmP[L]

        def s_stage(hh):
            pr, lo = hh // 2, 64 * (hh % 2)
            out = []
            for mb in range(2):
                ps, psb = banks[4 + 2 * (hh % 2) + mb]
                mm(ps[:, :], kt[lo:lo + 64, pr, 128 * mb:128 * mb + 128], mqn[lo:lo + 64, pr, :], True, True, [kb, B_mqn], [psb])
                out.append((ps, psb))
            return out

        nxt = s_stage(0)
        done = []
        for hh in range(4):
            curs = nxt
            po, pob = banks[hh]
            pts = []
            for mb in range(2):
                ps, psb = curs[mb]
                pt_, ptb = pT_ring.next()
                act(pt_[:], ps[:, :], AF.Exp, [psb, B_Mcol], [ptb], bias=Mcol[:, 1 + L:2 + L])
                pts.append((pt_, ptb))
            if hh + 1 < 4:
                nxt = s_stage(hh + 1)
            for mb in range(2):
                pt_, ptb = pts[mb]
                mm(po[0:65, :], vt[:, mb, hh, :], pt_[:], mb == 0, mb == 1, [vb, ptb], [pob])
            done.append((hh, po, pob))
        for i_, (hh, po, pob) in enumerate(done):
            oap, obuf = out_fn(hh)
            normalize_out(po, pob, oap, obuf, bank=banks[4 + i_])
            if after is not None:
                after(hh, oap, obuf)

    with ExitStack() as ps0:
        phase_reset()
        alloc_work(ps0)
        memt, B_memt = sb(ps0, "memt", [128, 2, D], F32)
        memsq, B_memsq = sb(ps0, "memsq", [128, D], F32)
        mss, B_mss = sb(ps0, "mss", [128, 2], F32)
        mnb, B_mnb = sb(ps0, "mnb", [128, D], F32)
        memn, B_memn = sb(ps0, "memn", [128, 2, D], F32)
        memnT, B_memnT = sb(ps0, "memnT", [128, KC, 256], BF16)
        wkv, B_wkv = sb(ps0, "wkv", [128, KC, 512], BF16)
        dma(ld_ring, mnb[:], memnorm_d[:, :], [], [B_mnb])
        for mb in range(2):
            dma(ld_ring, memt[:, mb, :], mem_in[128 * mb:128 * mb + 128, :], [], [B_memt])
        for mb in range(2):
            act(memsq[:], memt[:, mb, :], AF.Square, [B_memt], [B_memsq])
            vec("dve", (lambda mb: lambda e: e.reduce_sum(out=mss[:, mb:mb + 1], in_=memsq[:], axis=AX.X))(mb), [B_memsq], [B_mss])
        act(mss[:], mss[:], AF.Ln, [B_mss], [B_mss], bias=EPS, scale=1.0 / D)
        act(mss[:], mss[:], AF.Exp, [B_mss], [B_mss], scale=-0.5)
        for mb in range(2):
            vec("dve", (lambda mb: lambda e: e.scalar_tensor_tensor(out=memn[:, mb, :], in0=memt[:, mb, :], scalar=mss[:, mb:mb + 1], in1=mnb[:],
                                                                  op0=ALU.mult, op1=ALU.mult))(mb), [B_memt, B_mss, B_mnb], [B_memn])
        for mb in range(2):
            for kq in range(2):
                pt, pb = bank_ring.next()
                for kk in range(4):
                    k = 4 * kq + kk
                    vec("pe", (lambda pt, kk, mb, k: lambda e: e.transpose(out=pt[:, 128 * kk:128 * kk + 128], in_=memn[:, mb, 128 * k:128 * k + 128],
                                                                             identity=ident_f[:]))(pt, kk, mb, k),
                          [B_memn, B_ident_f], [pb])
                vec("dve", (lambda pt, kq, mb: lambda e: e.tensor_copy(out=memnT[:, 4 * kq:4 * kq + 4, 128 * mb:128 * mb + 128],
                                                                      in_=pt[:, :].rearrange("p (k t) -> p k t", k=4)))(pt, kq, mb), [pb], [B_memnT])
        for L in (0, 1):
            src = wb_memkv[L][:, :].rearrange("(k p) c -> p k c", p=128)
            dma(ld_ring, wkv[:], src, [dbuf("wmemkv%d" % L)], [B_wkv])
            kt, kb = kmT[L]
            vt, vb = VmP[L]
            vec("dve", (lambda vt: lambda e: e.memset(vt[:], 1.0))(vt), [], [vb])
            for pr in range(2):
                pt, pb = bank_ring.next()
                for k in range(KC):
                    mm(pt[:, 0:256], wkv[:, k, 128 * pr:128 * pr + 128], memnT[:, k, :], k == 0, k == KC - 1, [B_wkv, B_memnT], [pb])
                pair_norm(pt, pb, hcols[:, 3 + 2 * L:4 + 2 * L], kt[:, pr, :], kb, n=256)
            for mb in range(2):
                pt, pb = bank_ring.next()
                for k in range(KC):
                    mm(pt[:, 0:256], memnT[:, k, 128 * mb:128 * mb + 128], wkv[:, k, 256:512], k == 0, k == KC - 1, [B_wkv, B_memnT], [pb])
                vec("dve", (lambda vt, mb, pt: lambda e: e.tensor_copy(out=vt[:, mb, :, 0:64], in_=pt[:, 0:256].rearrange("p (h d) -> p h d", h=4)))(vt, mb, pt),
                    [pb], [vb])
    S.barrier()

    LZ, B_LZ = sb(es, "LZ", [128, 64, 12], F32)
    with ExitStack() as ps1:
        phase_reset()
        alloc_work(ps1)
        kn_ring = Ring([sb(ps1, "kn%d" % i, [128, T], BF16) for i in range(2)])
        vp_ring = Ring([sb(ps1, "vp%d" % i, [128, 12, 4, 65], BF16) for i in range(2)])
        mqn, B_mqn = sb(ps1, "mqn", [128, 2, T], BF16)
        mo_ring = Ring([sb(ps1, "mo%d" % i, [64, T], BF16) for i in range(2)])
        bfb, B_bfb = sb(ps1, "bfb", [128, 12], F32)
        dma(ld_ring, bfb[:], bf_d[:, :], [], [B_bfb])
        for t_, b_ in vp_ring.items:
            vec("dve", (lambda t_: lambda e: e.memset(t_[:], 1.0))(t_), [], [b_])

        def p1_pre_a(jj):
            j, own = jj // 2, jj % 2 == 0
            xT, B_xT = W["xT2"][jj % 2]
            xsrc = x_own if own else x_oth
            dma(ld_ring, xT[:], xsrc[j], [], [B_xT])
            norm_a((xT, B_xT))

        def p1_mid_a(jj):
            j, own = jj // 2, jj % 2 == 0
            xT, B_xT = W["xT2"][jj % 2]
            if own:
                dma(stb_ring, X[j], xT[:], [B_xT], [dbuf("X%d" % j)])
            norm_a((xT, B_xT))

        def p1_post(jj):
            j, own = jj // 2, jj % 2 == 0
            hT, B_hT = W["hT2"]
            slot0 = 4 * j if own else 32 + 4 * j
            kinds = [("k", 768, Ks)] + ([("q", 0, Qs)] if own else [])
            pend = []

            def flush_pair():
                pt, pb, nm, pr = pend.pop(0)
                kt_, kb_ = kn_ring.next()
                gcol = hcols[:, 1:2] if nm == "k" else hcols8[:, 0:1]
                pair_norm(pt, pb, gcol, kt_[:], kb_)
                for hh in range(2):
                    h = 2 * pr + hh
                    if nm == "k":
                        dap = Ks[h, :, 128 * slot0:128 * slot0 + 512]
                        dn = "Ks"
                    else:
                        dap = Qs[h, 0:64, 512 * j:512 * j + 512]
                        dn = "Qs"
                    dma(st_ring, dap, kt_[64 * hh:64 * hh + 64, :], [kb_], [dbuf(dn)])

            for nm, cbase, dst in kinds:
                for (c0, npair, p0) in ((cbase, 4, 0), (cbase + 512, 2, 4)):
                    tw, bw = load_wk(wb_fox, "wfox_%d" % c0, c0, 128 * npair)
                    for pi in range(npair):
                        pr = p0 + pi
                        pt, pb = bank_ring.next()
                        for k in range(KC):
                            mm(pt[:, :], tw[:, k, 128 * pi:128 * pi + 128], hT[:, k, :], k == 0, k == KC - 1, [bw, B_hT], [pb])
                        pend.append((pt, pb, nm, pr))
                        if len(pend) > 1:
                            flush_pair()
            mq_pend = []
            if own:
                tw, bw = load_wk(wb_fox, "wfox_2316", 2316, 256)
                for pr in range(2):
                    pt, pb = bank_ring.next()
                    for k in range(KC):
                        mm(pt[:, :], tw[:, k, 128 * pr:128 * pr + 128], hT[:, k, :], k == 0, k == KC - 1, [bw, B_hT], [pb])
                    mq_pend.append((pt, pb, pr))
                    if pr == 0:
                        flush_pair()
                for (pt, pb, pr) in mq_pend:
                    pair_norm(pt, pb, hcols8[:, 2:3], mqn[:, pr, :], B_mqn)
            else:
                flush_pair()
            tv1, bv1 = load_wk(wb_fox, "wfox_1536", 1536, 512)
            tv2, bv2 = load_wk(wb_fox, "wfox_2048", 2048, 268)
            vt_, vb_ = vp_ring.next()
            for bi in range(4):
                pa, pab = bank_ring.next()
                pb2, pbb = bank_ring.next()
                for k in range(KC):
                    mm(pa[:, :], hT[:, k, 128 * bi:128 * bi + 128], tv1[:, k, 0:512], k == 0, k == KC - 1, [bv1, B_hT], [pab])
                for k in range(KC):
                    mm(pb2[:, 0:268], hT[:, k, 128 * bi:128 * bi + 128], tv2[:, k, 0:268], k == 0, k == KC - 1, [bv2, B_hT], [pbb])
                act(vt_[:, 0:8, bi, 0:64], pa[:, :].rearrange("p (h d) -> p h d", h=8), AF.Copy, [pab], [vb_])
                vec("dve", (lambda vt_, bi, pb2: lambda e: e.tensor_copy(out=vt_[:, 8:12, bi, 0:64],
                                                                        in_=pb2[:, 0:256].rearrange("p (h d) -> p h d", h=4)))(vt_, bi, pb2),
                    [pbb], [vb_])
                vec("dve", (lambda bi, pb2, slot0: lambda e: e.tensor_tensor(out=LZ[:, slot0 + bi, :], in0=pb2[:, 256:268], in1=bfb[:], op=ALU.add))(bi, pb2, slot0),
                    [pbb, B_bfb], [B_LZ])
            dma(stb_ring, Vs[:, :, slot0:slot0 + 4, :].rearrange("h p s c -> p h s c"), vt_[:], [vb_], [dbuf("Vs")])
            if own:
                stores = []

                def out_fn(hh):
                    t_, b_ = mo_ring.next()
                    stores.append((hh, t_, b_))
                    return t_[:], b_
                mem_attn(0, mqn, B_mqn, out_fn, lambda hh, t_, b_: dma(st_ring, CATs[12 + hh, :, 512 * j:512 * j + 512], t_[:], [b_], [dbuf("CATs")]))

        if upto >= 1:
            run_tiles(2 * NT,
                      p1_pre_a,
                      lambda jj: norm_b(W["xT2"][jj % 2], 0, W["hT"]),
                      lambda jj, hook: ffn_ab((1, 0), hook),
                      lambda jj, hook, yp: ffn_y((1, 0), W["xT2"][jj % 2], hook, yp),
                      p1_mid_a,
                      lambda jj: norm_b(W["xT2"][jj % 2], 8, W["hT2"]),
                      p1_post, (1, 0))
    S.barrier()

    biasK, B_biasK = sb(es, "biasK", [128, 64, 12], F32)
    if upto >= 2:
        with ExitStack() as ps2:
            phase_reset()
            nl, B_nl = sb(ps2, "nl", [128, 768], F32)
            tot, B_tot = sb(ps2, "tot", [128, 64, 12], F32)
            pa_, B_pa = sb(ps2, "ppa", [128, 32, 12], F32)
            pb_, B_pb = sb(ps2, "ppb", [128, 32, 12], F32)
            pair, B_pair = sb(ps2, "pair", [128, 32, 12], F32)
            off, B_off = sb(ps2, "off", [128, 64, 12], F32)
            cq, B_cq = sb(ps2, "cq", [128, 12, 32], F32)
            cqr, B_cqr = sb(ps2, "cqr", [128, 3, 128], BF16)
            LZf = LZ[:].rearrange("p s h -> p (s h)")
            act(nl[:], LZf, AF.Exp, [B_LZ], [B_nl], scale=-1.0)
            act(nl[:], nl[:], AF.Ln, [B_nl], [B_nl], bias=1.0)
            pc1, pc1b = bank_ring.next()
            pc2, pc2b = bank_ring.next()
            pt1, pt1b = bank_ring.next()
            pt2, pt2b = bank_ring.next()
            mm(pc1[:, :], tri_f[:], nl[:, 0:512], True, True, [B_tri_f, B_nl], [pc1b])
            mm(pc2[:, 0:256], tri_f[:], nl[:, 512:768], True, True, [B_tri_f, B_nl], [pc2b])
            mm(pt1[:, :], ones_f[:], nl[:, 0:512], True, True, [B_ones_f, B_nl], [pt1b])
            mm(pt2[:, 0:256], ones_f[:], nl[:, 512:768], True, True, [B_ones_f, B_nl], [pt2b])
            totf = tot[:].rearrange("p s h -> p (s h)")
            vec("dve", lambda e: e.tensor_copy(out=totf[:, 0:512], in_=pt1[:, :]), [pt1b], [B_tot])
            vec("dve", lambda e: e.tensor_copy(out=totf[:, 512:768], in_=pt2[:, 0:256]), [pt2b], [B_tot])
            vec("dve", lambda e: e.tensor_tensor(out=pair[:], in0=tot[:, 0:32, :], in1=tot[:, 32:64, :], op=ALU.add), [B_tot], [B_pair])
            vec("dve", lambda e: e.tensor_copy(out=pa_[:], in_=pair[:]), [B_pair], [B_pa])
            cur, curb, nxt, nxtb = pa_, B_pa, pb_, B_pb
            d = 1
            while d < 32:
                vec("dve", (lambda cur, nxt, d: lambda e: e.tensor_copy(out=nxt[:, 0:d, :], in_=cur[:, 0:d, :]))(cur, nxt, d), [curb], [nxtb])
                vec("dve", (lambda cur, nxt, d: lambda e: e.tensor_tensor(out=nxt[:, d:32, :], in0=cur[:, d:32, :], in1=cur[:, 0:32 - d, :], op=ALU.add))(cur, nxt, d),
                    [curb], [nxtb])
                cur, curb, nxt, nxtb = nxt, nxtb, cur, curb
                d *= 2
            vec("dve", (lambda cur: lambda e: e.tensor_tensor(out=pair[:], in0=cur[:], in1=pair[:], op=ALU.subtract))(cur), [curb, B_pair], [B_pair])
            vec("dve", lambda e: e.scalar_tensor_tensor(out=off[:, 0:32, :], in0=tot[:, 32:64, :], scalar=pflag[:, 0:1], in1=pair[:], op0=ALU.mult, op1=ALU.add),
                [B_tot, B_pflag, B_pair], [B_off])
            vec("dve", lambda e: e.scalar_tensor_tensor(out=off[:, 32:64, :], in0=tot[:, 0:32, :], scalar=pflag[:, 1:2], in1=pair[:], op0=ALU.mult, op1=ALU.add),
                [B_tot, B_pflag, B_pair], [B_off])
            offf = off[:].rearrange("p s h -> p (s h)")
            bKf = biasK[:].rearrange("p s h -> p (s h)")
            vec("dve", lambda e: e.tensor_tensor(out=offf[:, 0:512], in0=pc1[:, :], in1=offf[:, 0:512], op=ALU.add), [pc1b, B_off], [B_off])
            vec("dve", lambda e: e.tensor_tensor(out=offf[:, 512:768], in0=pc2[:, 0:256], in1=offf[:, 512:768], op=ALU.add), [pc2b, B_off], [B_off])
            vec("dve", lambda e: e.tensor_scalar(out=bKf, in0=offf, scalar1=Mcol[:, 0:1], scalar2=None, op0=ALU.add), [B_off, B_Mcol], [B_biasK])
            if debug:
                dma(st_ring, dbg_bias[:, :], bKf, [B_biasK], [dbuf("dbg_bias")])
            vec("dve", lambda e: e.tensor_scalar_mul(out=cq[:], in0=off[:, 0:32, :].rearrange("p b h -> p h b"), scalar1=-1.0), [B_off], [B_cq])
            cqf = cq[:].rearrange("p h b -> p (h b)")
            for i in range(3):
                pt, pb = bank_ring.next()
                vec("pe", (lambda pt, i: lambda e: e.transpose(out=pt[:, 0:128], in_=cqf[:, 128 * i:128 * i + 128], identity=ident_f[:]))(pt, i),
                      [B_cq, B_ident_f], [pb])
                vec("dve", (lambda pt, i: lambda e: e.tensor_copy(out=cqr[:, i, :], in_=pt[:, 0:128]))(pt, i), [pb], [B_cqr])
            for h in range(12):
                i, r = h // 4, 32 * (h % 4)
                dma(st_ring, Qs[h, 64, :].rearrange("(b t) -> b t", t=128), cqr[r:r + 32, i, :], [B_cqr], [dbuf("Qs")])
    S.barrier()

    if upto >= 3:
        with ExitStack() as ps3:
            phase_reset()
            kT_ring = Ring([sb(ps3, "kT%d" % i, [65, 8192], BF16) for i in range(2)])
            qT_ring = Ring([sb(ps3, "qT%d" % i, [65, 4096], BF16) for i in range(2)])
            vh_ring = Ring([sb(ps3, "vh%d" % i, [128, 64, 65], BF16) for i in range(2)])
            ob_ring = Ring([sb(ps3, "ob%d" % i, [64, T], BF16) for i in range(2)])
            hd_dma = S.ring("hd", 6)
            for t_, b_ in kT_ring.items:
                vec("dve", (lambda t_: lambda e: e.memset(t_[64:65, :], 1.0))(t_), [], [b_])
            s_banks = Ring(banks[0:5])
            o_banks = Ring(banks[5:7])
            bank_ring.items = banks[7:8]
            steps = []
            for h in range(12):
                for m in range(NT):
                    nblk = 4 * m + 4
                    for i in range(nblk):
                        for own in (True, False):
                            steps.append((h, m, i, own, i == 0 and own, (i == nblk - 1) and (not own)))
            LA = 3
            cur = {}
            sinfo = {}

            def load_head(hn):
                kt_, kb_ = kT_ring.next()
                qt_, qb_ = qT_ring.next()
                vt_, vb_ = vh_ring.next()
                dma(hd_dma, kt_[0:64, :], Ks[hn], [dbuf("Ks")], [kb_])
                dma(hd_dma, qt_[:], Qs[hn], [dbuf("Qs")], [qb_])
                dma(hd_dma, vt_[:], Vs[hn], [dbuf("Vs")], [vb_])
                cur[hn] = (kt_, kb_, qt_, qb_, vt_, vb_)

            def stage_a(idx):
                h, m, i, own, first, last = steps[idx]
                kt_, kb_, qt_, qb_, vt_, vb_ = cur[h]
                slot = i if own else 32 + i
                c0 = 0 if i < 4 * m else 128 * (i - 4 * m)
                diag = i >= 4 * m
                ps_, psb = s_banks.next()
                mm(ps_[:, c0:512], kt_[0:65, 128 * slot:128 * slot + 128], qt_[0:65, 512 * m + c0:512 * m + 512], True, not diag,
                   [kb_, qb_], [psb])
                if diag:
                    mk, mkb = (trineg_b, B_trineg_b) if own else (othmask_b, B_othmask_b)
                    mm(ps_[:, c0:c0 + 128], ident_b[:], mk[:], False, True, [B_ident_b, mkb], [psb])
                sinfo[idx] = (ps_, psb, slot, c0)

            def stage_b(idx):
                h, m, i, own, first, last = steps[idx]
                kt_, kb_, qt_, qb_, vt_, vb_ = cur[h]
                ps_, psb, slot, c0 = sinfo.pop(idx)
                if first:
                    cur["o"] = o_banks.next()
                po, pob = cur["o"]
                pt_, ptb = pT_ring.next()
                act(pt_[:, c0:512], ps_[:, c0:512], AF.Exp, [psb, B_biasK], [ptb], bias=biasK[:, slot, h:h + 1])
                mm(po[0:65, c0:512], vt_[:, slot, :], pt_[:, c0:512], first, last, [vb_, ptb], [pob])
                if last:
                    ot, otb = ob_ring.next()
                    normalize_out(po, pob, ot[:], otb)
                    dma(st_ring, CATs[h, :, 512 * m:512 * m + 512], ot[:], [otb], [dbuf("CATs")], eng="sp")

            load_head(0)
            load_head(1)
            late_casts([cur[1][1], cur[1][3], cur[1][5]])
            for idx in range(len(steps) + LA):
                if idx < len(steps):
                    stage_a(idx)
                if idx >= LA:
                    stage_b(idx - LA)
                    hh, mm_, _, _, _, last_ = steps[idx - LA]
                    if last_ and mm_ == NT - 1 and hh + 2 < 12:
                        load_head(hh + 2)
            bank_ring.items = banks
    S.barrier()

    if upto >= 4:
        with ExitStack() as ps4:
            phase_reset()
            alloc_work(ps4)
            cat, B_cat = sb(ps4, "cat", [128, 8, T], BF16)

            def p4_pre_a(j):
                xT, B_xT = W["xT2"][j % 2]
                dma(ld_ring, xT[:], X[j], [dbuf("X%d" % j)], [B_xT])
                dma(ld_ring, cat[:], CATs[:, :, 512 * j:512 * j + 512].rearrange("(c two) p t -> (two p) c t", two=2), [dbuf("CATs")], [B_cat])
                for dp in range(4):
                    t0_, b = W["wk_ring"].next()
                    t = t0_[:, :, :].rearrange("p k c -> p (k c)")[:, 0:2048].rearrange("p (c n) -> p c n", n=256)
                    src = wb_o[0][:, 256 * dp:256 * dp + 256].rearrange("(c p) n -> p c n", p=128)
                    dma(wk_dma, t[:, 0:8, :], src, [dbuf("wo0_%d" % dp)], [b])
                    for dd in range(2):
                        do = 2 * dp + dd
                        py, pyb = bank_ring.next()
                        for c in range(8):
                            mm(py[:, :], t[:, c, 128 * dd:128 * dd + 128], cat[:, c, :], c == 0, c == 7, [b, B_cat], [pyb])
                        vec("dve", (lambda do, py: lambda e: e.tensor_tensor(out=xT[:, do, :], in0=py[:, :], in1=xT[:, do, :], op=ALU.add))(do, py),
                            [pyb, B_xT], [B_xT])
                norm_a((xT, B_xT))

            def p4_post(j):
                xT, B_xT = W["xT2"][j % 2]
                dma(stb_ring, X[j], xT[:], [B_xT], [dbuf("X%d" % j)])

            run_tiles(NT, p4_pre_a,
                      lambda j: norm_b(W["xT2"][j % 2], 16, W["hT"]),
                      lambda j, hook: ffn_ab((2, 0), hook),
                      lambda j, hook, yp: ffn_y((2, 0), W["xT2"][j % 2], hook, yp),
                      nop, nop, p4_post, (2, 0))
    S.barrier()

    if upto >= 5:
        with ExitStack() as ps5:
            phase_reset()
            alloc_work(ps5, nwk=4, share=True, nsilu=1)
            wsT, B_wsT = sb(ps5, "wsT", [128, 12, 128], BF16)
            wsl, B_wsl = W["rstd"][0][:, 0:128], W["rstd"][1]
            bsp, B_bsp = sb(ps5, "bsp", [128, 6, 128], F32)
            vgc, B_vgc = sb(ps5, "vgc", [128, 6], F32)
            uT, B_uT = sb(ps5, "uT", [128, 6, T], F32)
            vg_ring = [sb(ps5, "vg%d" % i, [128, 768], F32) for i in range(2)]
            vsq, B_vsq = sb(ps5, "vsq", [128, 768], BF16)
            vss_ring = [sb(ps5, "vss%d" % i, [128, 12], F32) for i in range(4)]
            vn, B_vn = sb(ps5, "vn", [128, 4, 768], BF16)
            tokc, B_tokc = sb(ps5, "tokc", [128, 6, T], BF16)
            gtmp, B_gtmp = W["lnt"]
            mqn5, B_mqn5 = sb(ps5, "mqn5", [128, 2, T], BF16)
            moc, B_moc = sb(ps5, "moc", [64, 4, T], BF16)
            dma(ld_ring, bsp[:], bs_d[:, :, :], [], [B_bsp])
            dma(ld_ring, vgc[:], vgain_d[:, :], [], [B_vgc])
            for g in range(12):
                dma(ld_ring, wsl[:], ws_d[g], [], [B_wsl])
                pt, pb = bank_ring.next()
                vec("pe", (lambda pt: lambda e: e.transpose(out=pt[:, 0:128], in_=wsl[:], identity=ident_f[:]))(pt), [B_wsl, B_ident_f], [pb])
                vec("dve", (lambda pt, g: lambda e: e.tensor_tensor(out=wsT[:, g, :], in0=pt[:, 0:128], in1=tri_f[:], op=ALU.mult))(pt, g), [pb, B_tri_f], [B_wsT])

            def p5_pre_a(j):
                xT, B_xT = W["xT2"][j % 2]
                dma(ld_ring, xT[:], X[j], [dbuf("X%d" % j)], [B_xT])
                norm_a((xT, B_xT))

            def p5_post(j):
                xT, B_xT = W["xT2"][j % 2]
                hT, B_hT = W["hT2"]
                tw, bw = load_wk(wb_gmlp, "wgmlp_1536", 1536, 256)
                for pr in range(2):
                    pt, pb = bank_ring.next()
                    for k in range(KC):
                        mm(pt[:, :], tw[:, k, 128 * pr:128 * pr + 128], hT[:, k, :], k == 0, k == KC - 1, [bw, B_hT], [pb])
                    pair_norm(pt, pb, hcols8[:, 4:5], mqn5[:, pr, :], B_mqn5)
                tv1, bv1 = load_wk(wb_gmlp, "wgmlp_768", 768, 512)
                tv2, bv2 = load_wk(wb_gmlp, "wgmlp_1280", 1280, 256)

                def v_s1(bi):
                    pa, pab = bank_ring.next()
                    pb2, pbb = bank_ring.next()
                    for k in range(KC):
                        mm(pa[:, :], hT[:, k, 128 * bi:128 * bi + 128], tv1[:, k, 0:512], k == 0, k == KC - 1, [bv1, B_hT], [pab])
                    for k in range(KC):
                        mm(pb2[:, 0:256], hT[:, k, 128 * bi:128 * bi + 128], tv2[:, k, 0:256], k == 0, k == KC - 1, [bv2, B_hT], [pbb])
                    vg, B_vg = vg_ring[bi % 2]
                    vss, B_vss = vss_ring[bi]
                    act(vg[:, 0:512], pa[:, :], AF.Gelu_apprx_tanh, [pab], [B_vg])
                    act(vg[:, 512:768], pb2[:, 0:256], AF.Gelu_apprx_tanh, [pbb], [B_vg])
                    act(vsq[:], vg[:], AF.Square, [B_vg], [B_vsq])
                    vec("dve", lambda e: e.reduce_sum(out=vss[:], in_=vsq[:].rearrange("p (g d) -> p g d", g=12), axis=AX.X), [B_vsq], [B_vss])

                def v_s2(bi):
                    vg, B_vg = vg_ring[bi % 2]
                    vss, B_vss = vss_ring[bi]
                    act(vss[:], vss[:], AF.Ln, [B_vss], [B_vss], bias=EPS, scale=1.0 / 64)
                    act(vss[:], vss[:], AF.Exp, [B_vss], [B_vss], scale=-0.5)
                    vec("dve", lambda e: e.tensor_tensor(out=vn[:, bi, :].rearrange("p (g d) -> p g d", g=12),
                                                         in0=vg[:].rearrange("p (g d) -> p g d", g=12),
                                                         in1=vss[:].unsqueeze(2).to_broadcast([128, 12, 64]), op=ALU.mult),
                        [B_vg, B_vss], [B_vn])

                v_s1(0)
                v_s1(1)
                v_s2(0)
                v_s1(2)
                v_s2(1)
                v_s1(3)
                v_s2(2)
                v_s2(3)
                for (c0, npair, p0) in ((0, 4, 0), (512, 2, 4)):
                    tw, bw = load_wk(wb_gmlp, "wgmlp_%d" % c0, c0, 128 * npair)
                    for pi in range(npair):
                        pr = p0 + pi
                        pt, pb = bank_ring.next()
                        for k in range(KC):
                            mm(pt[:, :], tw[:, k, 128 * pi:128 * pi + 128], hT[:, k, :], k == 0, k == KC - 1, [bw, B_hT], [pb])
                        act(uT[:, pr, :], pt[:, :], AF.Gelu_apprx_tanh, [pb], [B_uT])
                mem_attn(1, mqn5, B_mqn5, lambda hh: (moc[:, hh, :], B_moc))
                for pr in range(6):
                    pt, pb = bank_ring.next()
                    for bi in range(4):
                        for gg in range(2):
                            g = 2 * pr + gg
                            mm(pt[64 * gg:64 * gg + 64, 128 * bi:128 * bi + 128], vn[:, bi, 64 * g:64 * g + 64], wsT[:, g, :], True, True, [B_vn, B_wsT], [pb])
                    vec("dve", (lambda pt, pr: lambda e: e.scalar_tensor_tensor(out=gtmp[:].rearrange("p (b t) -> p b t", b=4),
                                                                               in0=pt[:, :].rearrange("p (b t) -> p b t", b=4),
                                                                               scalar=vgc[:, pr:pr + 1],
                                                                               in1=bsp[:, pr, :].unsqueeze(1).to_broadcast([128, 4, 128]),
                                                                               op0=ALU.mult, op1=ALU.add))(pt, pr),
                        [pb, B_bsp, B_vgc], [B_gtmp])
                    vec("dve", (lambda pr: lambda e: e.tensor_tensor(out=tokc[:, pr, :], in0=gtmp[:], in1=uT[:, pr, :], op=ALU.mult))(pr),
                        [B_gtmp, B_uT], [B_tokc])
                for dp in range(4):
                    t0_, b = W["wk_ring"].next()
                    t = t0_[:, :, :].rearrange("p k c -> p (k c)")[:, 0:2560].rearrange("p (c n) -> p c n", n=256)
                    src = wb_o[1][0:768, 256 * dp:256 * dp + 256].rearrange("(c p) n -> p c n", p=128)
                    dma(wk_dma, t[:, 0:6, :], src, [dbuf("wo1_%d" % dp)], [b])
                    src2 = wb_o[1][768:1024, 256 * dp:256 * dp + 256].rearrange("(c p) n -> p c n", p=64)
                    dma(wk_dma, t[0:64, 6:10, :], src2, [dbuf("wo1_%d" % dp)], [b])
                    for dd in range(2):
                        do = 2 * dp + dd
                        py, pyb = bank_ring.next()
                        for c in range(6):
                            mm(py[:, :], t[:, c, 128 * dd:128 * dd + 128], tokc[:, c, :], c == 0, False, [b, B_tokc], [pyb])
                        for c in range(4):
                            mm(py[:, :], t[0:64, 6 + c, 128 * dd:128 * dd + 128], moc[0:64, c, :], False, c == 3, [b, B_moc], [pyb])
                        vec("dve", (lambda do, py: lambda e: e.tensor_tensor(out=xT[:, do, :], in0=py[:, :], in1=xT[:, do, :], op=ALU.add))(do, py),
                            [pyb, B_xT], [B_xT])
                dma(stb_ring, X[j], xT[:], [B_xT], [dbuf("X%d" % j)])

            run_tiles(NT, p5_pre_a,
                      lambda j: norm_b(W["xT2"][j % 2], 24, W["hT"]),
                      lambda j, hook: ffn_ab((1, 1), hook),
                      lambda j, hook, yp: ffn_y((1, 1), W["xT2"][j % 2], hook, yp),
                      lambda j: norm_a(W["xT2"][j % 2]),
                      lambda j: norm_b(W["xT2"][j % 2], 32, W["hT2"]),
                      p5_post, (1, 1))
    S.barrier()

    if upto >= 6:
        with ExitStack() as ps6:
            phase_reset()
            alloc_work(ps6)

            def p6_pre_a(j):
                xT, B_xT = W["xT2"][j % 2]
                dma(ld_ring, xT[:], X[j], [dbuf("X%d" % j)], [B_xT])
                norm_a((xT, B_xT))

            def p6_post(j):
                xT, B_xT = W["xT2"][j % 2]
                dma(stb_ring, out_d[j], xT[:], [B_xT], [dbuf("out")])

            run_tiles(NT, p6_pre_a,
                      lambda j: norm_b(W["xT2"][j % 2], 40, W["hT"]),
                      lambda j, hook: ffn_ab((2, 1), hook),
                      lambda j, hook, yp: ffn_y((2, 1), W["xT2"][j % 2], hook, yp),
                      nop, nop, p6_post, (2, 1))
    S.barrier()

    S.finalize()
    with nc.Block() as block:
        @block.tensor
        def _(eng):
            S.emit_engine("pe", eng)

        @block.scalar
        def _(eng):
            S.emit_engine("act", eng)

        @block.vector
        def _(eng):
            S.emit_engine("dve", eng)

        @block.gpsimd
        def _(eng):
            S.emit_engine("pool", eng)

        @block.sync
        def _(eng):
            S.emit_engine("sp", eng)
    es.close()
    return nc


def _host_inputs(inp, c):
    b, p = c // 2, c % 2
    f32 = np.float32
    x = np.asarray(inp["x"], f32)[b].reshape(64, 128, D)
    m = {}
    m["x_own"] = np.ascontiguousarray(x[p::2].reshape(NT, T, KC, 128).transpose(0, 3, 2, 1))
    m["x_oth"] = np.ascontiguousarray(x[1 - p::2].reshape(NT, T, KC, 128).transpose(0, 3, 2, 1))
    m["mem"] = np.ascontiguousarray(np.asarray(inp["mem"], f32)[b])
    ffn_w = {1: (inp["ffn1_w_in"], inp["ffn1_w_out"]), 2: (inp["ffn2_w_in"], inp["ffn2_w_out"])}
    for i in (1, 2):
        for L in (0, 1):
            m["ffn%d_w_in_%d" % (i, L)] = np.ascontiguousarray(np.asarray(ffn_w[i][0], f32)[L])
            m["ffn%d_w_out_%d" % (i, L)] = np.ascontiguousarray(np.asarray(ffn_w[i][1], f32)[L])
    for L in (0, 1):
        m["w_out_%d" % L] = np.ascontiguousarray(np.asarray(inp["w_out"], f32)[L])
        m["mem_w_kv_%d" % L] = np.ascontiguousarray(np.asarray(inp["mem_w_kv"], f32)[L])
    m["fox_w_in"] = np.ascontiguousarray(np.asarray(inp["fox_w_in"], f32)[0])
    m["gmlp_w_in"] = np.ascontiguousarray(np.asarray(inp["gmlp_w_in"], f32)[0])
    g = []
    for L in (0, 1):
        for nm in ("norm_ffn1", "norm_mix", "norm_ffn2"):
            g.append(np.asarray(inp[nm], f32)[L].reshape(8, 128).T)
    m["gains"] = np.ascontiguousarray(np.concatenate(g, axis=1))
    fq = np.asarray(inp["fox_q_norm"], f32)[0]
    fk = np.asarray(inp["fox_k_norm"], f32)[0]
    mq = np.asarray(inp["mem_q_norm"], f32)
    mk = np.asarray(inp["mem_k_norm"], f32)
    cols = [fq, fk, mq[0], mk[0], mq[1], mk[1], fq, fk]
    m["hcols"] = np.ascontiguousarray(np.stack([np.tile(v, 2) for v in cols], axis=1))
    m["hrows"] = np.ascontiguousarray(np.broadcast_to(np.concatenate([fq, fk])[None, :], (128, 128)))
    m["hrows2"] = np.ascontiguousarray(np.broadcast_to(np.concatenate([mq[0], mk[0], mq[1], mk[1]])[None, :], (128, 256)))
    m["bf_b"] = np.ascontiguousarray(np.broadcast_to(np.asarray(inp["fox_b_f"], f32)[0][None, :], (128, 12)))
    m["memnorm_b"] = np.ascontiguousarray(np.broadcast_to(np.asarray(inp["mem_norm"], f32)[None, :], (128, D)))
    m["vgain_c"] = np.ascontiguousarray(np.asarray(inp["gmlp_v_norm"], f32)[0].reshape(6, 128).T)
    m["gmlp_w_s"] = np.ascontiguousarray(np.asarray(inp["gmlp_w_s"], f32)[0])
    bs = np.asarray(inp["gmlp_b_s"], f32)[0]
    m["bs_pair"] = np.ascontiguousarray(np.repeat(bs.reshape(6, 2, 1, 128), 64, axis=2).reshape(6, 128, 128).transpose(1, 0, 2))
    m["ident"] = np.eye(128, dtype=f32)
    s_idx = np.arange(128)[:, None]
    t_idx = np.arange(128)[None, :]
    m["tri01"] = (s_idx <= t_idx).astype(f32)
    m["trineg"] = np.where(s_idx <= t_idx, 0.0, NEG).astype(f32)
    m["othmask"] = np.full((128, 128), NEG if p == 0 else 0.0, f32)
    bd = np.zeros((128, 128), f32)
    bd[0:64, 0:64] = 1.0
    bd[64:128, 64:128] = 1.0
    m["bdiag"] = bd
    m["pflag"] = np.ascontiguousarray(np.broadcast_to(np.array([p, 1 - p], f32)[None, :], (128, 2)))
    return m


_NC_CACHE = {}


def kernel(**inputs):
    if "nc" not in _NC_CACHE:
        _NC_CACHE["nc"] = build()
    nc = _NC_CACHE["nc"]
    in_maps = [_host_inputs(inputs, c) for c in range(8)]
    res = run_bass_kernel_spmd(nc, in_maps, core_ids=list(range(8)))
    out = np.zeros((4, 64, 128, D), np.float32)
    for c in range(8):
        b, p = c // 2, c % 2
        o = np.asarray(res.results[c]["out"], np.float32)
        out[b, p::2] = o.transpose(0, 3, 2, 1).reshape(32, 128, D)
    return out.reshape(4, 8192, D)
```
